# Optimizing a Trainium2 kernel written in Bass

```python
import jax, jax.numpy as jnp
from jax import lax
import numpy as np

D_MODEL = 2048
BATCH = 2
SEQ = 8192
DEPTH = 1
DEC_BATCH = 1
DEC_SEQ = 8192
PAST_LEN = 128

D_RWKV = D_MODEL // 2
D_CONV = D_MODEL - D_RWKV
HEAD_DIM = 64
N_HEADS = D_RWKV // HEAD_DIM
DECAY_LORA = 64
AAA_LORA = 64
GATE_LORA = 160
CONV_WIDTH = 31
CONV_HALF = CONV_WIDTH // 2
D_FF = 4 * D_MODEL
RMS_EPS = 1e-6
LN_EPS = 1e-5
GN_EPS = 64e-5
L2_EPS = 1e-12
RWKV_COLS = 3 * D_RWKV + 2 * DECAY_LORA + 2 * AAA_LORA + GATE_LORA
IN_COLS = RWKV_COLS + 2 * D_CONV

kernel_name = "hymba_rwkv7_conformer_bidir_encoder"


def _rms(x, g):
    xf = x.astype(jnp.float32)
    y = xf * lax.rsqrt(jnp.mean(xf * xf, axis=-1, keepdims=True) + RMS_EPS)
    return (y * g.astype(jnp.float32)).astype(x.dtype)


def _centred_shift(z, mu_prev, mu_next):
    zp = jnp.pad(z, ((0, 0), (1, 0), (0, 0)))[:, :-1]
    zn = jnp.pad(z, ((0, 0), (0, 1), (0, 0)))[:, 1:]
    return z + mu_prev * (zp - z) + mu_next * (zn - z)


def _wkv_scan(r, w, k, v, kk, a, reverse):
    B, T, H, N = r.shape

    def step(S, inp):
        r_t, w_t, k_t, v_t, kk_t, a_t = inp
        sa = jnp.einsum('bhvk,bhk->bhv', S, kk_t)
        S = (S * w_t[:, :, None, :]
             - jnp.einsum('bhv,bhk->bhvk', sa, kk_t * a_t)
             + jnp.einsum('bhv,bhk->bhvk', v_t, k_t))
        y = jnp.einsum('bhvk,bhk->bhv', S, r_t)
        return S, y

    xs = tuple(jnp.swapaxes(t, 0, 1) for t in (r, w, k, v, kk, a))
    S0 = jnp.zeros((B, H, N, N), jnp.float32)
    _, y = lax.scan(step, S0, xs, reverse=reverse)
    return jnp.swapaxes(y, 0, 1)


def _heads(t):
    B, T, _ = t.shape
    return t.reshape(B, T, N_HEADS, HEAD_DIM)


def _rwkv_direction(r, k, v, kk, xw, xa, w0, w2, a0, a2, k_a, reverse):
    w_log = -jax.nn.softplus(-(w0 + jnp.tanh(xw) @ w2)) - 0.5
    w = jnp.exp(-jnp.exp(w_log))
    a = jax.nn.sigmoid(a0 + xa @ a2)
    k_mod = k * (1.0 + (a - 1.0) * k_a)
    return _wkv_scan(_heads(r), _heads(w), _heads(k_mod), _heads(v), kk, _heads(a), reverse)


def _layer(x, g_pre_mix, w_in, mu_prev, mu_next, w0_f, w2_f, w0_b, w2_b,
           a0_f, a2_f, a0_b, a2_b, g2, k_k, k_a, r_k, gn_w, gn_b,
           dw_w, dw_b, cln_w, cln_b, w_out, g_post_mix, g_pre_mlp,
           w_up, w_down, g_post_mlp):
    B, T, _ = x.shape
    h = _rms(x, g_pre_mix)
    z = h @ w_in
    z_rwkv = _centred_shift(z[..., :RWKV_COLS], mu_prev, mu_next).astype(jnp.float32)
    z_conv = z[..., RWKV_COLS:]

    o = 0
    r = z_rwkv[..., o:o + D_RWKV]; o += D_RWKV
    k = z_rwkv[..., o:o + D_RWKV]; o += D_RWKV
    v = z_rwkv[..., o:o + D_RWKV]; o += D_RWKV
    xw_f = z_rwkv[..., o:o + DECAY_LORA]; o += DECAY_LORA
    xw_b = z_rwkv[..., o:o + DECAY_LORA]; o += DECAY_LORA
    xa_f = z_rwkv[..., o:o + AAA_LORA]; o += AAA_LORA
    xa_b = z_rwkv[..., o:o + AAA_LORA]; o += AAA_LORA
    xg = z_rwkv[..., o:o + GATE_LORA]

    f32 = lambda t: t.astype(jnp.float32)
    g = jax.nn.sigmoid(xg) @ f32(g2)
    kk = _heads(k * f32(k_k))
    kk = kk / jnp.maximum(jnp.sqrt(jnp.sum(kk * kk, axis=-1, keepdims=True)), L2_EPS)

    y_f = _rwkv_direction(r, k, v, kk, xw_f, xa_f, f32(w0_f), f32(w2_f), f32(a0_f), f32(a2_f), f32(k_a), False)
    y_b = _rwkv_direction(r, k, v, kk, xw_b, xa_b, f32(w0_b), f32(w2_b), f32(a0_b), f32(a2_b), f32(k_a), True)
    y = y_f + y_b

    mu = jnp.mean(y, axis=-1, keepdims=True)
    var = jnp.mean(jnp.square(y - mu), axis=-1, keepdims=True)
    y = ((y - mu) * lax.rsqrt(var + GN_EPS)).reshape(B, T, D_RWKV) * f32(gn_w) + f32(gn_b)
    rh, kh, vh = _heads(r), _heads(k), _heads(v)
    bonus = (jnp.sum(rh * kh * f32(r_k), axis=-1, keepdims=True) * vh).reshape(B, T, D_RWKV)
    o_rwkv = ((y + bonus) * g).astype(x.dtype)

    u = z_conv[..., :D_CONV] * jax.nn.sigmoid(z_conv[..., D_CONV:])
    c = lax.conv_general_dilated(u, dw_w[:, None, :], window_strides=(1,),
                                 padding=((CONV_HALF, CONV_HALF),),
                                 dimension_numbers=('NWC', 'WIO', 'NWC'),
                                 feature_group_count=D_CONV) + dw_b
    cf = c.astype(jnp.float32)
    cm = jnp.mean(cf, axis=-1, keepdims=True)
    cv = jnp.mean(jnp.square(cf - cm), axis=-1, keepdims=True)
    cf = (cf - cm) * lax.rsqrt(cv + LN_EPS) * f32(cln_w) + f32(cln_b)
    o_conv = jax.nn.silu(cf).astype(x.dtype)

    mix = jnp.concatenate([o_rwkv, o_conv], axis=-1) @ w_out
    x = x + _rms(mix, g_post_mix)

    hm = _rms(x, g_pre_mlp)
    ff = jnp.square(jax.nn.relu(hm @ w_up)) @ w_down
    return x + _rms(ff, g_post_mlp)


def _trunk(x, weights):
    for l in range(DEPTH):
        x = _layer(x, *[p[l] for p in weights])
    return x


def setup_inputs(seed: int = 0) -> dict:
    key = jax.random.key(seed)
    ks = jax.random.split(key, 32)
    nrm = lambda k, shape, s: jax.random.normal(k, shape, jnp.float32) * s
    L = DEPTH
    return {
        "x_prompt": nrm(ks[0], (BATCH, SEQ, D_MODEL), 1.0),
        "x_sample": nrm(ks[1], (DEC_BATCH, DEC_SEQ, D_MODEL), 1.0),
        "g_pre_mix": 1.0 + nrm(ks[2], (L, D_MODEL), 0.02),
        "w_in": nrm(ks[3], (L, D_MODEL, IN_COLS), D_MODEL ** -0.5),
        "mu_prev": jax.random.uniform(ks[4], (L, RWKV_COLS), jnp.float32, 0.0, 0.5),
        "mu_next": jax.random.uniform(ks[5], (L, RWKV_COLS), jnp.float32, 0.0, 0.5),
        "w0_f": jax.random.uniform(ks[6], (L, D_RWKV), jnp.float32, -1.5, 0.5),
        "w2_f": nrm(ks[7], (L, DECAY_LORA, D_RWKV), 0.1 * DECAY_LORA ** -0.5),
        "w0_b": jax.random.uniform(ks[8], (L, D_RWKV), jnp.float32, -1.5, 0.5),
        "w2_b": nrm(ks[9], (L, DECAY_LORA, D_RWKV), 0.1 * DECAY_LORA ** -0.5),
        "a0_f": nrm(ks[10], (L, D_RWKV), 0.1),
        "a2_f": nrm(ks[11], (L, AAA_LORA, D_RWKV), 0.1 * AAA_LORA ** -0.5),
        "a0_b": nrm(ks[12], (L, D_RWKV), 0.1),
        "a2_b": nrm(ks[13], (L, AAA_LORA, D_RWKV), 0.1 * AAA_LORA ** -0.5),
        "g2": nrm(ks[14], (L, GATE_LORA, D_RWKV), GATE_LORA ** -0.5),
        "k_k": 0.85 + nrm(ks[15], (L, D_RWKV), 0.02),
        "k_a": 1.0 + nrm(ks[16], (L, D_RWKV), 0.02),
        "r_k": nrm(ks[17], (L, N_HEADS, HEAD_DIM), 0.1),
        "gn_w": 1.0 + nrm(ks[18], (L, D_RWKV), 0.02),
        "gn_b": nrm(ks[19], (L, D_RWKV), 0.01),
        "dw_w": nrm(ks[20], (L, CONV_WIDTH, D_CONV), CONV_WIDTH ** -0.5),
        "dw_b": nrm(ks[21], (L, D_CONV), 0.01),
        "cln_w": 1.0 + nrm(ks[22], (L, D_CONV), 0.02),
        "cln_b": nrm(ks[23], (L, D_CONV), 0.01),
        "w_out": nrm(ks[24], (L, D_MODEL, D_MODEL), D_MODEL ** -0.5),
        "g_post_mix": 1.0 + nrm(ks[25], (L, D_MODEL), 0.02),
        "g_pre_mlp": 1.0 + nrm(ks[26], (L, D_MODEL), 0.02),
        "w_up": nrm(ks[27], (L, D_MODEL, D_FF), D_MODEL ** -0.5),
        "w_down": nrm(ks[28], (L, D_FF, D_MODEL), D_FF ** -0.5),
        "g_post_mlp": 1.0 + nrm(ks[29], (L, D_MODEL), 0.02),
    }


def reference(x_prompt, x_sample, g_pre_mix, w_in, mu_prev, mu_next, w0_f, w2_f,
              w0_b, w2_b, a0_f, a2_f, a0_b, a2_b, g2, k_k, k_a, r_k, gn_w, gn_b,
              dw_w, dw_b, cln_w, cln_b, w_out, g_post_mix, g_pre_mlp, w_up,
              w_down, g_post_mlp):
    weights = (g_pre_mix, w_in, mu_prev, mu_next, w0_f, w2_f, w0_b, w2_b,
               a0_f, a2_f, a0_b, a2_b, g2, k_k, k_a, r_k, gn_w, gn_b,
               dw_w, dw_b, cln_w, cln_b, w_out, g_post_mix, g_pre_mlp,
               w_up, w_down, g_post_mlp)
    y_prompt = _trunk(x_prompt, weights)
    y_sample = _trunk(x_sample, weights)
    return (y_prompt, y_sample)
```

```python
import contextlib
import numpy as np
import concourse.bass as bass
import concourse.mybir as mybir
from concourse.bass_utils import run_bass_kernel_spmd

F32 = mybir.dt.float32
BF16 = mybir.dt.bfloat16
AF = mybir.ActivationFunctionType
ALU = mybir.AluOpType
AX = mybir.AxisListType

D = 2048
DR = 1024
NH = 16
HD = 64
RW = 3488
INC = 5536
DFF = 8192
CW = 31
CDEC = 0.6065306597126334
HIL = 8

ENGS = ("pe", "act", "dve", "pool", "sp")
NDS = 20
EPOCH = 30000


class Buf:
    _n = 0

    psum_ids = set()

    def __init__(self, t, nparts=1, name=None, psum=False):
        self.t = t
        self.nparts = nparts
        Buf._n += 1
        self.id = Buf._n
        self.name = name
        if psum:
            Buf.psum_ids.add(self.id)

    def __getitem__(self, idx):
        return self.t[idx]


def _parts(spec):
    if isinstance(spec, Buf):
        return [(spec.id, p) for p in range(spec.nparts)]
    b, p = spec
    if isinstance(p, int):
        return [(b.id, p)]
    return [(b.id, q) for q in p]


class Prog:
    def __init__(self, nc):
        self.nc = nc
        self.ops = []
        self.last_w = {}
        self.readers = {}
        self.cnt = {e: 0 for e in ENGS}
        self.dcnt = {e: 0 for e in ENGS}
        self.done = []
        self.known = {e: {} for e in ENGS}
        self.sems = {}
        self.flushed = 0
        self.stack = contextlib.ExitStack()
        self.total = {e: 0 for e in ENGS}

    def op(self, eng, fn, reads=(), writes=(), dma=False):
        i = len(self.ops)
        deps = set()
        for r in reads:
            for p in _parts(r):
                w = self.last_w.get(p)
                if w is not None:
                    deps.add((w, False))
                if p[0] in Buf.psum_ids:
                    for rd in self.readers.get(p, ()):
                        deps.add((rd, True))
        for wspec in writes:
            for p in _parts(wspec):
                w = self.last_w.get(p)
                if w is not None:
                    deps.add((w, False))
                for rd in self.readers.get(p, ()):
                    deps.add((rd, True))
        for r in reads:
            for p in _parts(r):
                self.readers.setdefault(p, []).append(i)
        for wspec in writes:
            for p in _parts(wspec):
                self.last_w[p] = i
                self.readers[p] = []
        if dma:
            j = self.dcnt[eng]
            self.dcnt[eng] += 1
            done = (("d", eng, j % NDS), 16 * (j // NDS + 1))
        else:
            j = self.cnt[eng]
            self.cnt[eng] += 1
            done = (("c", eng, j // EPOCH), j % EPOCH + 1)
        self.done.append(done)
        self.ops.append(dict(eng=eng, fn=fn, deps=deps, dma=dma, j=j))
        return i

    def dma(self, out, in_, reads=(), writes=(), eng="sp", **kw):
        return self.op(eng, lambda e: e.dma_start(out=out, in_=in_, **kw), reads, writes, dma=True)

    def sem(self, key):
        s = self.sems.get(key)
        if s is None:
            s = self.stack.enter_context(self.nc.semaphore("s_" + "_".join(str(k) for k in key)))
            self.sems[key] = s
        return s

    def flush(self):
        nc = self.nc
        ops = self.ops
        per_eng = {e: [] for e in ENGS}
        for i in range(self.flushed, len(ops)):
            o = ops[i]
            e = o["eng"]
            need = {}
            for d, war in o["deps"]:
                od = ops[d]
                if od["eng"] == e and not od["dma"]:
                    if e == "pe" or e == "sp" or war:
                        continue
                key, val = self.done[d]
                if need.get(key, 0) < val:
                    need[key] = val
            if o["dma"] and o["j"] >= NDS:
                key = ("d", e, o["j"] % NDS)
                val = 16 * (o["j"] // NDS)
                if need.get(key, 0) < val:
                    need[key] = val
            waits = []
            for key, val in need.items():
                if self.known[e].get(key, 0) >= val:
                    continue
                self.known[e][key] = val
                waits.append((self.sem(key), val))
            key, val = self.done[i]
            per_eng[e].append((waits, o["fn"], self.sem(key), o["dma"]))
            o["fn"] = None
        self.flushed = len(ops)
        handles = dict(pe="tensor", act="scalar", dve="vector", pool="gpsimd", sp="sync")
        dcnt = dict(self.dcnt)
        with nc.Block() as block:
            for e in ENGS:
                lst = per_eng[e]
                if not lst:
                    continue
                self.total[e] += len(lst)

                def body(eh, lst=lst, e0=e):
                    for waits, fn, s, isdma in lst:
                        for ws, wv in waits:
                            eh.wait_ge(ws, wv)
                        fn(eh).then_inc(s, 16 if isdma else 1)
                    for k in range(min(NDS, dcnt[e0])):
                        nuse = (dcnt[e0] - k + NDS - 1) // NDS
                        eh.wait_ge(self.sem(("d", e0, k)), 16 * nuse)
                        self.known[e0][("d", e0, k)] = 16 * nuse

                getattr(block, handles[e])(body)


def build(T, debug=False, nphase=99, cut=99, hcut=99, TOWN=None):
    nc = bass.Bass("TRN2", target_bir_lowering=False)
    P = Prog(nc)
    NCH = T // 128
    TOWN = T if TOWN is None else TOWN
    NOWN = TOWN // 128

    def din(name, shape):
        return nc.dram_tensor(name, list(shape), F32, kind="ExternalInput").ap()

    def dscr(name, shape):
        return nc.dram_tensor(name, list(shape), F32, kind="ExternalOutput" if debug else "Internal").ap()

    x = din("x", [T, D])
    w_in = din("w_in", [D, INC])
    w_out = din("w_out", [D, D])
    w_up = din("w_up", [D, DFF])
    w_down = din("w_down", [DFF, D])
    consts_d = din("consts", [128, 1024])
    gcols_d = din("gcols", [128, 32])
    mu_d = din("mu", [128, 2 * RW])
    vecs_d = din("vecs", [128, 7 * DR])
    gpost_d = din("gpost", [128, 2 * D])
    lora_d = din("lora", [2, 128, DR])
    misc_d = din("misc", [2, 128, DR])
    g2a_d = din("g2a", [128, DR])
    dwT_d = din("dwT", [128, 8 * CW])
    dwb_d = din("dwb", [128, 8])
    y_out = nc.dram_tensor("y", [TOWN, D], F32, kind="ExternalOutput").ap()

    Z = dscr("Zs", [T + 2, INC])
    YF = dscr("YFs", [T, DR])
    O = dscr("Os", [T, D])
    UT = dscr("UTs", [DR, T + 30])
    Zb, YFb, Ob, UTb = Buf(Z, NCH + 2), Buf(YF, NCH), Buf(O, 2 * NCH), Buf(UT, 1)
    wbf = {}
    for nm, src in (("w_in", w_in), ("w_out", w_out), ("w_up", w_up), ("w_down", w_down)):
        t = nc.dram_tensor(nm + "_bf", list(src.shape), BF16, kind="Internal").ap()
        wbf[nm] = (t, Buf(t, 1, nm + "_bf"), src)

    gst = contextlib.ExitStack()
    with gst, nc.allow_low_precision("bf16 matmul operands, fp32 PSUM accumulation"):
        def sbg(name, shape, nparts=1):
            return Buf(gst.enter_context(nc.sbuf_tensor("sb_" + name, list(shape), F32)), nparts, name)

        banks = [Buf(gst.enter_context(nc.psum_tensor("bank%d" % i, [128, 512], F32)), 1, "bank%d" % i, psum=True) for i in range(8)]
        bstate = [0]

        def next_ps():
            b = banks[bstate[0] % 8]
            bstate[0] += 1
            return b

        for nm in ("w_in", "w_out", "w_up", "w_down"):
            t, tb, src = wbf[nm]
            rows = src.shape[0]
            step = 128 if src.shape[1] > 4096 else 256
            for r0 in range(0, rows, step):
                P.dma(t[r0:r0 + step, :], src[r0:r0 + step, :], writes=[tb], eng="pool")
        consts = sbg("consts", [128, 1024])
        gcols = sbg("gcols", [128, 32])
        P.dma(consts[:], consts_d, writes=[consts])
        P.dma(gcols[:], gcols_d, writes=[gcols])
        ident = consts[:, 0:128]
        ones = consts.t[:, 896:1024]

        def mask2(di):
            return consts[:, 128 + 384 * di: 384 + 384 * di]

        def maskT(di):
            return consts[:, 384 + 384 * di: 512 + 384 * di]

        def tri_le(di):
            return consts[:, 256 + 384 * di: 384 + 384 * di]

        ecnt = [0]

        def evac(out, in_, reads, writes, eng=None):
            if eng is None:
                eng = "act" if ecnt[0] % 2 == 0 else "dve"
                ecnt[0] += 1
            if eng == "act":
                P.op("act", lambda e: e.copy(out=out, in_=in_), reads, writes)
            else:
                P.op(eng, lambda e: e.tensor_copy(out=out, in_=in_), reads, writes)

        def tt(eng, out, in0, in1, op, reads, writes):
            P.op(eng, lambda e: e.tensor_tensor(out=out, in0=in0, in1=in1, op=op), reads, writes)

        def ts(eng, out, in0, s1, s2, op0, op1, reads, writes):
            if op1 is None:
                P.op(eng, lambda e: e.tensor_scalar(out=out, in0=in0, scalar1=s1, scalar2=None, op0=op0), reads, writes)
            else:
                P.op(eng, lambda e: e.tensor_scalar(out=out, in0=in0, scalar1=s1, scalar2=s2, op0=op0, op1=op1), reads, writes)

        def stt(out, in0, scalar, in1, op0, op1, reads, writes):
            P.op("dve", lambda e: e.scalar_tensor_tensor(out=out, in0=in0, scalar=scalar, in1=in1, op0=op0, op1=op1), reads, writes)

        def act(out, in_, func, reads, writes, scale=None, accum=None):
            kw = {}
            if scale is not None:
                kw["scale"] = scale
            if accum is not None:
                kw["accum_out"] = accum
            P.op("act", lambda e: e.activation(out=out, in_=in_, func=func, **kw), reads, writes)

        def mm(out, lhsT, rhs, start, stop, reads, writes):
            P.op("pe", lambda e: e.matmul(out, lhsT=lhsT, rhs=rhs, start=start, stop=stop), reads, writes)

        def tr(out, in_, reads, writes):
            P.op("pe", lambda e: e.transpose(out=out, in_=in_, identity=ident), list(reads) + [consts], writes)

        def rstd_from_ss(ss, rs, n, eps, reads_extra=()):
            ts("dve", rs[:], ss[:], 1.0 / n, eps, ALU.mult, ALU.add, [ss], [rs])
            act(rs[:], rs[:], AF.Sqrt, [rs], [rs])
            P.op("dve", lambda e: e.reciprocal(out=rs[:], in_=rs[:]), [rs], [rs])

        TB = min(512, T)
        nsub = TB // 128
        with contextlib.ExitStack() as st:
            def sb(name, shape, nparts=1, dt=F32):
                return Buf(st.enter_context(nc.sbuf_tensor("sb_" + name, list(shape), dt)), nparts, name)
            xb = [sb("p1x%d" % i, [128, D]) for i in range(2)]
            junk = sb("p1junk", [128, D])
            ssb = [sb("p1ss%d" % i, [128, 1]) for i in range(2)]
            rsb = [sb("p1rs%d" % i, [128, 1]) for i in range(2)]
            hT = sb("p1hT", [128, 16, TB], nsub, BF16)
            wb = [sb("p1w%d" % i, [128, 16, 512], 1, BF16) for i in range(3)]
            wlist = []
            for _b in range(T // TB):
                _n0 = 0
                while _n0 < INC:
                    _nw = min(512, INC - _n0)
                    wlist.append((_n0, _nw))
                    _n0 += _nw
            issued = [0]

            def need_w(i):
                while issued[0] <= min(i + 2, len(wlist) - 1):
                    j = issued[0]
                    a, b = wlist[j]
                    P.dma(wb[j % 3][:, :, 0:b], wbf["w_in"][0][:, a:a + b].rearrange("(c p) n -> p c n", p=128), reads=[wbf["w_in"][1]], writes=[wb[j % 3]])
                    issued[0] += 1
                return wb[i % 3]
            zsb = [sb("p1z%d" % i, [128, 512]) for i in range(2)]
            zero = sb("p1zero", [1, 1384])
            P.op("pool", lambda e: e.memset(zero[:], 0.0), [], [zero])
            for q4 in range(4):
                P.dma(Z[0:1, q4 * 1384:(q4 + 1) * 1384], zero[:], reads=[zero], writes=[(Zb, 0)])
                P.dma(Z[T + 1:T + 2, q4 * 1384:(q4 + 1) * 1384], zero[:], reads=[zero], writes=[(Zb, NCH + 1)])
            wk = 0
            zk = 0
            xk = 0
            for blk in range(T // TB):
                for s in range(nsub):
                    xt, ss, rs = xb[xk % 2], ssb[xk % 2], rsb[xk % 2]
                    xk += 1
                    r0 = blk * TB + s * 128
                    P.dma(xt[:], x[r0:r0 + 128, :], writes=[xt])
                    act(junk[:], xt[:], AF.Square, [xt], [junk, ss], accum=ss[:])
                    rstd_from_ss(ss, rs, D, 1e-6)
                    ts("dve", xt[:], xt[:], rs[:], None, ALU.mult, None, [xt, rs], [xt])
                    for c4 in range(4):
                        ps = next_ps()
                        for c in range(4):
                            tr(ps[:, c * 128:(c + 1) * 128], xt[:, (c4 * 4 + c) * 128:(c4 * 4 + c + 1) * 128], [xt], [ps])
                        for c in range(4):
                            cc = c4 * 4 + c
                            ts("dve", hT[:, cc, s * 128:(s + 1) * 128], ps[:, c * 128:(c + 1) * 128], gcols[:, cc:cc + 1], None,
                               ALU.mult, None, [ps, gcols], [(hT, s)])
                n0 = 0
                while n0 < INC:
                    nw = min(512, INC - n0)
                    wt = need_w(wk)
                    wk += 1
                    for s in range(nsub):
                        ps = next_ps()
                        for c in range(16):
                            mm(ps[:, 0:nw], hT[:, c, s * 128:(s + 1) * 128], wt[:, c, 0:nw], c == 0, c == 15, [(hT, s), wt], [ps])
                        zt = zsb[zk % 2]
                        zk += 1
                        evac(zt[:, 0:nw], ps[:, 0:nw], [ps], [zt])
                        ch = (blk * TB + s * 128) // 128
                        r0 = 1 + blk * TB + s * 128
                        P.dma(Z[r0:r0 + 128, n0:n0 + nw], zt[:, 0:nw], reads=[zt], writes=[(Zb, 1 + ch)], eng="pool")
                    n0 += nw
            P.flush()

        for di in range(2 if nphase >= 3 else (1 if nphase >= 2 else 0)):
            with contextlib.ExitStack() as st:
                def sb(name, shape, nparts=1, dt=F32):
                    return Buf(st.enter_context(nc.sbuf_tensor("sb_" + name, list(shape), dt)), nparts, name)
                pf = "s%d" % di
                mu = sb(pf + "mu", [128, 2 * RW])
                vecs = sb(pf + "vecs", [128, 5 * DR])
                lora = sb(pf + "lora", [128, DR])
                misc = sb(pf + "misc", [128, DR])
                g2a = sb(pf + "g2a", [128, DR])
                P.dma(mu[:], mu_d, writes=[mu])
                P.dma(vecs[:], vecs_d[:, 0:5 * DR], writes=[vecs])
                P.dma(lora[:], lora_d[di], writes=[lora])
                P.dma(misc[:], misc_d[di], writes=[misc])
                P.dma(g2a[:], g2a_d, writes=[g2a])
                kk_b, ka_b, rk_b, gnw_b, gnb_b = [vecs[:, i * DR:(i + 1) * DR] for i in range(5)]
                stg = [sb(pf + "stg%d" % i, [128, DR]) for i in range(3)]
                zsl = [sb(pf + "zs%d" % i, [128, RW]) for i in range(2)]
                tl = [sb(pf + "t%d" % i, [128, DR]) for i in range(8)]
                t_sig, t_a, t_kk, t_km, t_b, t_e, t_e2 = tl[:7]
                Yt = Buf(tl[7].t, NH, pf + "Yt")
                t_At, t_Bt, t_Kt, t_Rt, t_Bp, t_Kp, t_vb = [sb(pf + "tb%d" % i, [128, DR], 1, BF16) for i in range(7)]
                identb = sb(pf + "identb", [128, 128], 1, BF16)
                P.op("pool", lambda e, identb=identb: e.tensor_copy(out=identb[:], in_=ident), [consts], [identb])
                LT = sb(pf + "LT", [128, 128])
                LTT = sb(pf + "LTT", [128, 128])
                sm = sb(pf + "sm", [128, 64])
                Wc = sb(pf + "Wc", [64, 16])
                SG = sb(pf + "SG", [128, 160])
                sgT = sb(pf + "sgT", [128, 256])
                ST = sb(pf + "ST", [64, DR], NH, BF16)
                featT = [sb(pf + "fT%d" % i, [128, 512], 1, BF16) for i in range(max(2, HIL // 2))]
                MAs = [sb(pf + "MA%d" % i, [128, 256], 1, BF16) for i in range(HIL)]
                MBs = [sb(pf + "MB%d" % i, [128, 256], 1, BF16) for i in range(HIL)]
                PTa = [sb(pf + "PTa%d" % i, [128, 128], 1, BF16) for i in range(HIL)]
                PP = [[sb(pf + "PP%d_%d" % (i, j), [128, 256], 1, BF16) for j in range(2)] for i in range(HIL)]
                ZZ = [[sb(pf + "ZZ%d_%d" % (i, j), [128, 128], 1, BF16) for j in range(2)] for i in range(HIL)]
                GTs = [sb(pf + "GT%d" % i, [64, 64], 1, BF16) for i in range(HIL)]
                RhTs = [sb(pf + "RhT%d" % i, [64, 128], 1, BF16) for i in range(HIL)]
                P.op("pool", lambda e, ST=ST: e.memset(ST[:], 0.0), [], [ST])
                if debug:
                    print("SBUF remaining in sweep", di, nc.sbuf_bytes_remaining)
                m2, mT, tle = mask2(di), maskT(di), tri_le(di)
                blocks = [(0, 1024), (1024, 2048), (2048, 3072), (3072, RW)]
                order = range(NOWN) if di == 0 else range(NCH - 1, -1, -1)
                order = list(order)

                def mix_gen(ch, zs):
                    t0 = ch * 128
                    for (c0, c1) in blocks:
                        n = c1 - c0
                        zc, zp, zn = stg
                        P.dma(zc[:, 0:n], Z[1 + t0:1 + t0 + 128, c0:c1], reads=[(Zb, 1 + ch)], writes=[zc])
                        P.dma(zp[:, 0:n], Z[t0:t0 + 128, c0:c1], reads=[(Zb, [ch, 1 + ch])], writes=[zp])
                        P.dma(zn[:, 0:n], Z[2 + t0:2 + t0 + 128, c0:c1], reads=[(Zb, [1 + ch, 2 + ch])], writes=[zn])
                        yield
                        tt("pool", zp[:, 0:n], zp[:, 0:n], zc[:, 0:n], ALU.subtract, [zp, zc], [zp])
                        tt("dve", zn[:, 0:n], zn[:, 0:n], zc[:, 0:n], ALU.subtract, [zn, zc], [zn])
                        yield
                        tt("pool", zp[:, 0:n], zp[:, 0:n], mu[:, c0:c1], ALU.mult, [zp, mu], [zp])
                        tt("dve", zn[:, 0:n], zn[:, 0:n], mu[:, RW + c0:RW + c1], ALU.mult, [zn, mu], [zn])
                        yield
                        tt("pool", zs[:, c0:c1], zc[:, 0:n], zp[:, 0:n], ALU.add, [zc, zp], [zs])
                        tt("dve", zs[:, c0:c1], zs[:, c0:c1], zn[:, 0:n], ALU.add, [zs, zn], [zs])
                        yield

                for idx, ch in enumerate(order):
                    t0 = ch * 128
                    own = ch < NOWN
                    zs = zsl[idx % 2]
                    if idx == 0:
                        for _ in mix_gen(ch, zs):
                            pass
                    nxt_mix = mix_gen(order[idx + 1], zsl[(idx + 1) % 2]) if idx + 1 < len(order) else None
                    r_, k_, v_ = zs[:, 0:DR], zs[:, DR:2 * DR], zs[:, 2 * DR:3 * DR]
                    xw = zs[:, 3072 + 64 * di:3136 + 64 * di]
                    xa = zs[:, 3200 + 64 * di:3264 + 64 * di]
                    xg = zs[:, 3328:3488]
                    if cut <= 1:
                        continue
                    act(LT[:, 0:64], xw, AF.Tanh, [zs], [LT])
                    P.op("act", lambda e, xa=xa: e.copy(out=LT[:, 64:128], in_=xa), [zs], [LT])
                    ps = next_ps()
                    tr(ps[:, 0:128], LT[:], [LT], [ps])
                    evac(LTT[:], ps[:, 0:128], [ps], [LTT])
                    for half in range(2):
                        hsl = slice(half * 512, (half + 1) * 512)
                        ps = next_ps()
                        mm(ps[:, :], LTT[0:64, :], lora[0:64, hsl], True, False, [LTT, lora], [ps])
                        mm(ps[:, :], ones[32:33, :], misc[32:33, hsl], False, True, [consts, misc], [ps])
                        act(t_sig[:, hsl], ps[:, :], AF.Sigmoid, [ps], [t_sig])
                        ps = next_ps()
                        mm(ps[:, :], LTT[64:128, :], lora[64:128, hsl], True, False, [LTT, lora], [ps])
                        mm(ps[:, :], ones[64:65, :], misc[64:65, hsl], False, True, [consts, misc], [ps])
                        act(t_a[:, hsl], ps[:, :], AF.Sigmoid, [ps], [t_a])
                    if cut <= 2:
                        continue
                    tt("dve", t_kk[:], k_, kk_b, ALU.mult, [zs, vecs], [t_kk])
                    tt("dve", t_e[:], t_kk[:], t_kk[:], ALU.mult, [t_kk], [t_e])
                    P.op("dve", lambda e: e.tensor_reduce(out=sm[:, 0:16], in_=t_e[:].rearrange("p (h k) -> p h k", k=HD), axis=AX.X, op=ALU.add), [t_e], [sm])
                    act(sm[:, 0:16], sm[:, 0:16], AF.Sqrt, [sm], [sm])
                    ts("dve", sm[:, 0:16], sm[:, 0:16], 1e-12, None, ALU.max, None, [sm], [sm])
                    P.op("dve", lambda e: e.reciprocal(out=sm[:, 0:16], in_=sm[:, 0:16]), [sm], [sm])
                    tt("dve", t_kk[:].rearrange("p (h k) -> p h k", k=HD), t_kk[:].rearrange("p (h k) -> p h k", k=HD),
                       sm[:, 0:16].unsqueeze(2).to_broadcast([128, NH, HD]), ALU.mult, [t_kk, sm], [t_kk])
                    if cut <= 3:
                        continue
                    stt(t_km[:], t_a[:], -1.0, ka_b, ALU.add, ALU.mult, [t_a, vecs], [t_km])
                    stt(t_km[:], t_km[:], 1.0, k_, ALU.add, ALU.mult, [t_km, zs], [t_km])
                    tt("dve", t_b[:], t_kk[:], t_a[:], ALU.mult, [t_kk, t_a], [t_b])
                    if cut <= 4:
                        continue
                    for half in range(2):
                        hsl = slice(half * 512, (half + 1) * 512)
                        ps = next_ps()
                        mm(ps[:, :], tle, t_sig[:, hsl], True, True, [consts, t_sig], [ps])
                        act(t_e[:, hsl], ps[:, :], AF.Exp, [ps], [t_e], scale=-CDEC)
                        tt("dve", t_Rt[:, hsl], r_[:, hsl], t_e[:, hsl], ALU.mult, [zs, t_e], [t_Rt])
                        act(t_e[:, hsl], ps[:, :], AF.Exp, [ps, t_Rt], [t_e], scale=CDEC)
                        tt("dve", t_Bt[:, hsl], t_b[:, hsl], t_e[:, hsl], ALU.mult, [t_b, t_e], [t_Bt])
                        tt("dve", t_Kt[:, hsl], t_km[:, hsl], t_e[:, hsl], ALU.mult, [t_km, t_e], [t_Kt])
                        tt("dve", t_e2[:, hsl], ps[:, :], t_sig[:, hsl], ALU.subtract, [ps, t_sig], [t_e2])
                        act(t_e[:, hsl], t_e2[:, hsl], AF.Exp, [t_e2, t_Bt, t_Kt], [t_e], scale=-CDEC)
                        stt(t_At[:, hsl], t_kk[:, hsl], -1.0, t_e[:, hsl], ALU.mult, ALU.mult, [t_kk, t_e], [t_At])
                        ps = next_ps()
                        mm(ps[:, :], mT, t_sig[:, hsl], True, True, [consts, t_sig], [ps])
                        act(t_e[:, hsl], ps[:, :], AF.Exp, [ps, t_At], [t_e], scale=-CDEC)
                        tt("dve", t_Bp[:, hsl], t_b[:, hsl], t_e[:, hsl], ALU.mult, [t_b, t_e], [t_Bp])
                        tt("dve", t_Kp[:, hsl], t_km[:, hsl], t_e[:, hsl], ALU.mult, [t_km, t_e], [t_Kp])
                    if cut <= 5:
                        continue
                    P.op("pool", lambda e, v_=v_: e.tensor_copy(out=t_vb[:], in_=v_), [zs], [t_vb])
                    ps = next_ps()
                    for h in range(NH):
                        mm(ps[0:64, h:h + 1], t_sig[:, h * 64:(h + 1) * 64], ones[:, 0:1], True, True, [t_sig, consts], [ps])
                    act(Wc[:], ps[0:64, 0:16], AF.Exp, [ps], [Wc], scale=-CDEC)
                    if di == 1 and own:
                        act(SG[:], xg, AF.Sigmoid, [zs], [SG])
                        ps = next_ps()
                        tr(ps[:, 0:128], SG[:, 0:128], [SG], [ps])
                        tr(ps[0:32, 128:256], SG[:, 128:160], [SG], [ps])
                        evac(sgT[:], ps[:, 0:256], [ps], [sgT])
                        tt("pool", t_e[:], r_, k_, ALU.mult, [zs, t_Kp, t_Bp], [t_e])
                        tt("pool", t_e[:], t_e[:], rk_b, ALU.mult, [t_e, vecs], [t_e])
                        P.op("dve", lambda e: e.tensor_reduce(out=sm[:, 16:32], in_=t_e[:].rearrange("p (h k) -> p h k", k=HD), axis=AX.X, op=ALU.add), [t_e], [sm])
                    if cut <= 6:
                        continue
                    pend = []
                    for hp in range(NH // 2):
                        cs = slice(hp * 128, (hp + 1) * 128)
                        fT = featT[hp % max(2, HIL // 2)]
                        ps = next_ps()
                        psb = ps[:, 0:256].bitcast(BF16)
                        for j, src in enumerate((t_At, t_Rt, t_Bt, t_Kt)):
                            P.op("pe", lambda e, j=j, src=src, psb=psb, cs=cs: e.transpose(out=psb[:, j * 128:(j + 1) * 128], in_=src[:, cs], identity=identb[:]), [src, identb], [ps])
                        evac(fT[:], psb, [ps], [fT])
                        if hcut <= 1:
                            continue
                        def head_gen(hh, hp=hp, fT=fT):
                            h = 2 * hp + hh
                            pb = 64 * hh
                            hs = slice(h * 64, (h + 1) * 64)
                            q = h % HIL
                            MA, MB, PT0, GT, RhT = MAs[q], MBs[q], PTa[q], GTs[q], RhTs[q]
                            AtT = fT[pb:pb + 64, 0:128]
                            BtT = fT[pb:pb + 64, 256:384]
                            KtT = fT[pb:pb + 64, 384:512]
                            AR = fT[pb:pb + 64, 0:256]
                            vh = t_vb[:, hs]
                            psA = next_ps()
                            mm(psA[:, 0:256], KtT, AR, True, True, [fT], [psA])
                            psB = next_ps()
                            mm(psB[:, 0:256], BtT, AR, True, True, [fT], [psB])
                            mm(psB[:, 256:384], AtT, BtT, True, True, [fT], [psB])
                            if hcut <= 2:
                                return
                            tt("dve", MA[:], psA[:, 0:256], m2, ALU.mult, [psA, consts], [MA])
                            tt("dve", MB[:], psB[:, 0:256], m2, ALU.mult, [psB, consts], [MB])
                            tt("dve", PT0[:], psB[:, 256:384], mT, ALU.mult, [psB, consts], [PT0])
                            yield
                            if hcut <= 3:
                                return
                            psZ = next_ps()
                            mm(psZ[:, 0:64], MA[:, 0:128], vh, True, True, [MA, t_vb], [psZ])
                            Zc = ZZ[q][0]
                            P.op("pool", lambda e, Zc=Zc, hs=hs: e.tensor_copy(out=Zc[:, 0:64], in_=t_At[:, hs]), [t_At], [Zc])
                            evac(Zc[:, 64:128], psZ[:, 0:64], [psZ], [Zc], eng="act")
                            yield
                            if hcut <= 4:
                                return
                            Pc, PTc = MB[:, 0:128], PT0[:]
                            Pbuf, PTbuf = MB, PT0
                            for i in range(7):
                                psz = next_ps()
                                mm(psz[:, 0:128], Pc, Zc[:], True, True, [Pbuf, Zc], [psz])
                                Zn = ZZ[q][(i + 1) % 2]
                                tt("dve", Zn[:], Zc[:], psz[:, 0:128], ALU.add, [Zc, psz], [Zn])
                                Zc = Zn
                                yield
                                if i < 6:
                                    psp = next_ps()
                                    mm(psp[:, 0:128], PTc, Pc, True, True, [Pbuf, PTbuf], [psp])
                                    nb = PP[q][i % 2]
                                    if i < 5:
                                        mm(psp[:, 128:256], Pc, PTc, True, True, [Pbuf, PTbuf], [psp])
                                        evac(nb[:, 0:256], psp[:, 0:256], [psp], [nb], eng="act")
                                    else:
                                        evac(nb[:, 0:128], psp[:, 0:128], [psp], [nb], eng="act")
                                    Pc, PTc = nb[:, 0:128], nb[:, 128:256]
                                    Pbuf = PTbuf = nb
                                    yield
                            if hcut <= 5:
                                return
                            Ah, U0 = Zc[:, 0:64], Zc[:, 64:128]
                            psG = next_ps()
                            mm(psG[0:64, 0:64], Ah, t_Bp[:, hs], True, True, [Zc, t_Bp], [psG])
                            if hcut == 51:
                                return
                            if own:
                                psR = next_ps()
                                mm(psR[0:64, 128:256], Ah, MB[:, 128:256], True, False, [Zc, MB], [psR])
                                mm(psR[0:64, 128:256], t_Rt[:, hs], identb[:], False, True, [t_Rt, identb], [psR])
                            if hcut == 52:
                                return
                            stt(GT[:], consts[0:64, 0:64], Wc[:, h:h + 1], psG[0:64, 0:64], ALU.mult, ALU.add, [consts, Wc, psG], [GT])
                            if hcut == 53:
                                return
                            if own:
                                evac(RhT[:], psR[0:64, 128:256], [psR], [RhT], eng="act")
                            yield
                            if hcut <= 6:
                                return
                            if own:
                                psY = next_ps()
                                mm(psY[:, 0:64], MB[:, 128:256], U0, True, False, [MB, Zc], [psY])
                                mm(psY[:, 0:64], MA[:, 128:256], vh, False, False, [MA, t_vb], [psY])
                                mm(psY[:, 0:64], RhT[:], ST[0:64, hs], False, True, [RhT, (ST, h)], [psY])
                            if hcut <= 7:
                                return
                            psS = next_ps()
                            mm(psS[0:64, 128:192], GT[:], ST[0:64, hs], True, False, [GT, (ST, h)], [psS])
                            mm(psS[0:64, 128:192], t_Kp[:, hs], vh, False, False, [t_Kp, t_vb], [psS])
                            mm(psS[0:64, 128:192], t_Bp[:, hs], U0, False, True, [t_Bp, Zc], [psS])
                            if own:
                                evac(Yt[:, hs], psY[:, 0:64], [psY], [(Yt, h)], eng="act")
                            evac(ST[0:64, hs], psS[0:64, 128:192], [psS], [(ST, h)], eng="dve")
                        pend.append(head_gen(0))
                        pend.append(head_gen(1))
                        if len(pend) < HIL and hp != NH // 2 - 1:
                            continue
                        gens = pend
                        pend = []
                        if nxt_mix is not None:
                            gens.append(nxt_mix)
                            nxt_mix = None
                        while gens:
                            for g_ in list(gens):
                                try:
                                    next(g_)
                                except StopIteration:
                                    gens.remove(g_)
                    if nxt_mix is not None:
                        for _ in nxt_mix:
                            pass
                    if cut <= 8:
                        continue
                    if not own:
                        continue
                    if di == 0:
                        P.dma(YF[t0:t0 + 128, :], Yt[:], reads=[Yt], writes=[(YFb, ch)], eng="pool")
                    else:
                        t_yf, t_g, t_y, t_sq = t_sig, t_a, t_kk, t_km
                        P.dma(t_yf[:], YF[t0:t0 + 128, :], reads=[(YFb, ch)], writes=[t_yf])
                        for half in range(2):
                            hsl = slice(half * 512, (half + 1) * 512)
                            ps = next_ps()
                            mm(ps[:, :], sgT[:, 0:128], g2a[:, hsl], True, False, [sgT, g2a], [ps])
                            mm(ps[:, :], sgT[0:32, 128:256], misc[0:32, hsl], False, True, [sgT, misc], [ps])
                            evac(t_g[:, hsl], ps[:, :], [ps], [t_g], eng="act")
                        tt("dve", t_y[:], Yt[:], t_yf[:], ALU.add, [Yt, t_yf], [t_y])
                        v3 = lambda t: t[:].rearrange("p (h k) -> p h k", k=HD)
                        bc = lambda a: a.unsqueeze(2).to_broadcast([128, NH, HD])
                        P.op("dve", lambda e: e.tensor_reduce(out=sm[:, 32:48], in_=v3(t_y), axis=AX.X, op=ALU.add), [t_y], [sm])
                        ts("dve", sm[:, 32:48], sm[:, 32:48], 1.0 / HD, None, ALU.mult, None, [sm], [sm])
                        tt("dve", v3(t_y), v3(t_y), bc(sm[:, 32:48]), ALU.subtract, [t_y, sm], [t_y])
                        tt("pool", t_sq[:], t_y[:], t_y[:], ALU.mult, [t_y], [t_sq])
                        P.op("dve", lambda e: e.tensor_reduce(out=sm[:, 48:64], in_=v3(t_sq), axis=AX.X, op=ALU.add), [t_sq], [sm])
                        ts("dve", sm[:, 48:64], sm[:, 48:64], 1.0 / HD, 64e-5, ALU.mult, ALU.add, [sm], [sm])
                        act(sm[:, 48:64], sm[:, 48:64], AF.Sqrt, [sm], [sm])
                        P.op("dve", lambda e: e.reciprocal(out=sm[:, 48:64], in_=sm[:, 48:64]), [sm], [sm])
                        tt("dve", v3(t_y), v3(t_y), bc(sm[:, 48:64]), ALU.mult, [t_y, sm], [t_y])
                        tt("pool", t_y[:], t_y[:], gnw_b, ALU.mult, [t_y, vecs], [t_y])
                        tt("pool", t_y[:], t_y[:], gnb_b, ALU.add, [t_y, vecs], [t_y])
                        tt("dve", v3(t_sq), zs[:, 2 * DR:3 * DR].rearrange("p (h k) -> p h k", k=HD), bc(sm[:, 16:32]), ALU.mult, [zs, sm, t_sq], [t_sq])
                        tt("pool", t_y[:], t_y[:], t_sq[:], ALU.add, [t_y, t_sq], [t_y])
                        tt("dve", t_y[:], t_y[:], t_g[:], ALU.mult, [t_y, t_g], [t_y])
                        P.dma(O[t0:t0 + 128, 0:DR], t_y[:], reads=[t_y], writes=[(Ob, 2 * ch)], eng="pool")
                P.flush()

        for _ in range(1 if nphase >= 4 else 0):
          with contextlib.ExitStack() as st:
            def sb(name, shape, nparts=1):
                return Buf(st.enter_context(nc.sbuf_tensor("sb_" + name, list(shape), F32)), nparts, name)
            zcv = [sb("c_zcv%d" % i, [128, 2048]) for i in range(2)]
            uu = [sb("c_u%d" % i, [128, DR]) for i in range(2)]
            uT = [sb("c_uT%d" % i, [128, 8, 128]) for i in range(2)]
            zpad = sb("c_zpad", [128, 8, 15])
            P.op("pool", lambda e: e.memset(zpad[:], 0.0), [], [zpad])
            P.dma(UT[:, 0:15].rearrange("(j p) t -> p j t", p=128), zpad[:], reads=[zpad], writes=[UTb])
            P.dma(UT[:, T + 15:T + 30].rearrange("(j p) t -> p j t", p=128), zpad[:], reads=[zpad], writes=[UTb])
            for ch in range(min(NCH, NOWN + 1)):
                t0 = ch * 128
                zt, u, uTt = zcv[ch % 2], uu[ch % 2], uT[ch % 2]
                P.dma(zt[:], Z[1 + t0:1 + t0 + 128, RW:INC], reads=[(Zb, 1 + ch)], writes=[zt])
                act(zt[:, DR:2 * DR], zt[:, DR:2 * DR], AF.Sigmoid, [zt], [zt])
                tt("dve", u[:], zt[:, 0:DR], zt[:, DR:2 * DR], ALU.mult, [zt], [u])
                for j4 in range(2):
                    ps = next_ps()
                    for j in range(4):
                        jj = j4 * 4 + j
                        tr(ps[:, j * 128:(j + 1) * 128], u[:, jj * 128:(jj + 1) * 128], [u], [ps])
                    evac(uTt[:, j4 * 4:(j4 + 1) * 4, :], ps[:, :].rearrange("p (j t) -> p j t", t=128), [ps], [uTt])
                P.dma(UT[:, 15 + t0:15 + t0 + 128].rearrange("(j p) t -> p j t", p=128), uTt[:], reads=[uTt], writes=[UTb], eng="pool")
            P.flush()
        for _ in range(1 if nphase >= 5 else 0):
          with contextlib.ExitStack() as st:
            def sb(name, shape, nparts=1):
                return Buf(st.enter_context(nc.sbuf_tensor("sb_" + name, list(shape), F32)), nparts, name)
            TBc = min(512, TOWN)
            nsc = TBc // 128
            dwT = sb("c_dwT", [128, 8 * CW])
            dwb = sb("c_dwb", [128, 8])
            cvec = sb("c_vec", [128, 2 * DR])
            P.dma(dwT[:], dwT_d, writes=[dwT])
            P.dma(dwb[:], dwb_d, writes=[dwb])
            P.dma(cvec[:], vecs_d[:, 5 * DR:7 * DR], writes=[cvec])
            ub = [sb("c_ub%d" % i, [128, TBc + 30]) for i in range(2)]
            acc = [sb("c_acc%d" % i, [128, TBc]) for i in range(2)]
            ctok = [sb("c_ct%d" % i, [128, DR]) for i in range(nsc)]
            stats = sb("c_st", [128, 12])
            mv = sb("c_mv", [128, 2])
            rsd = sb("c_rs", [128, 1])
            k2 = 0
            for blk in range(TOWN // TBc):
                b0 = blk * TBc
                for j in range(8):
                    ut, ac = ub[k2 % 2], acc[k2 % 2]
                    k2 += 1
                    P.dma(ut[:], UT[j * 128:(j + 1) * 128, b0:b0 + TBc + 30], reads=[UTb], writes=[ut])
                    ts("dve", ac[:], ut[:, 0:TBc], dwT[:, j * CW:j * CW + 1], dwb[:, j:j + 1], ALU.mult, ALU.add, [ut, dwT, dwb], [ac])
                    for tap in range(1, CW):
                        stt(ac[:], ut[:, tap:tap + TBc], dwT[:, j * CW + tap:j * CW + tap + 1], ac[:], ALU.mult, ALU.add, [ut, dwT, ac], [ac])
                    ps = next_ps()
                    for s in range(nsc):
                        tr(ps[:, s * 128:(s + 1) * 128], ac[:, s * 128:(s + 1) * 128], [ac], [ps])
                    for s in range(nsc):
                        evac(ctok[s][:, j * 128:(j + 1) * 128], ps[:, s * 128:(s + 1) * 128], [ps], [ctok[s]], eng="act")
                for s in range(nsc):
                    ct = ctok[s]
                    ch = (b0 + s * 128) // 128
                    for hf in range(2):
                        P.op("dve", lambda e, ct=ct, hf=hf: e.bn_stats(out=stats[:, hf * 6:(hf + 1) * 6], in_=ct[:, hf * 512:(hf + 1) * 512]), [ct], [stats])
                    P.op("dve", lambda e: e.bn_aggr(out=mv[:], in_=stats[:]), [stats], [mv])
                    ts("dve", rsd[:], mv[:, 1:2], 1e-5, None, ALU.add, None, [mv], [rsd])
                    act(rsd[:], rsd[:], AF.Sqrt, [rsd], [rsd])
                    P.op("dve", lambda e: e.reciprocal(out=rsd[:], in_=rsd[:]), [rsd], [rsd])
                    ts("dve", ct[:], ct[:], mv[:, 0:1], rsd[:], ALU.subtract, ALU.mult, [ct, mv, rsd], [ct])
                    tt("pool", ct[:], ct[:], cvec[:, 0:DR], ALU.mult, [ct, cvec], [ct])
                    tt("pool", ct[:], ct[:], cvec[:, DR:2 * DR], ALU.add, [ct, cvec], [ct])
                    act(ct[:], ct[:], AF.Silu, [ct], [ct])
                    P.dma(O[b0 + s * 128:b0 + (s + 1) * 128, DR:2 * DR], ct[:], reads=[ct], writes=[(Ob, 2 * ch + 1)], eng="pool")
            P.flush()

        for _ in range(1 if nphase >= 6 else 0):
          with contextlib.ExitStack() as st:
            def sb(name, shape, nparts=1, dt=F32):
                return Buf(st.enter_context(nc.sbuf_tensor("sb_" + name, list(shape), dt)), nparts, name)
            TB5 = min(512, TOWN)
            ns5 = TB5 // 128
            WN = 256
            gpost = sb("m_gpost", [128, 2 * D])
            P.dma(gpost[:], gpost_d, writes=[gpost])
            o_t = sb("m_ot", [128, D])
            junk = o_t
            oT = sb("m_oT", [128, 16, TB5], ns5, BF16)
            hmT = oT
            wb = [sb("m_w%d" % i, [128, 16, WN], 1, BF16) for i in range(3)]
            wlist = []
            for _b in range(TOWN // TB5):
                for _n in range(D // WN):
                    wlist.append(("w_out", wbf["w_out"][0][:, _n * WN:(_n + 1) * WN]))
                for _f in range(DFF // WN):
                    wlist.append(("w_up", wbf["w_up"][0][:, _f * WN:(_f + 1) * WN]))
                for _n in range(D // WN):
                    for _f4 in range(4):
                        wlist.append(("w_down", wbf["w_down"][0][_f4 * 2048:(_f4 + 1) * 2048, _n * WN:(_n + 1) * WN]))
            issued = [0]

            def need_w(i):
                while issued[0] <= min(i + 2, len(wlist) - 1):
                    j = issued[0]
                    P.dma(wb[j % 3][:], wlist[j][1].rearrange("(c p) n -> p c n", p=128), reads=[wbf[wlist[j][0]][1]], writes=[wb[j % 3]])
                    issued[0] += 1
                return wb[i % 3]
            mix = [sb("m_mix%d" % i, [128, D]) for i in range(ns5)]
            x1 = [sb("m_x1%d" % i, [128, D]) for i in range(ns5)]
            sss = [sb("m_ss%d" % i, [128, 1]) for i in range(4)]
            rss = [sb("m_rs%d" % i, [128, 1]) for i in range(4)]
            actT = sb("m_actT", [128, 64, TB5], 64, BF16)
            rl = [sb("m_rl%d" % i, [128, TB5]) for i in range(2)]
            wk = 0
            sk = 0
            rk = 0
            for blk in range(TOWN // TB5):
                b0 = blk * TB5
                for s in range(ns5):
                    ch = (b0 + s * 128) // 128
                    P.dma(o_t[:], O[b0 + s * 128:b0 + (s + 1) * 128, :], reads=[(Ob, [2 * ch, 2 * ch + 1])], writes=[o_t])
                    P.dma(x1[s][:], x[b0 + s * 128:b0 + (s + 1) * 128, :], writes=[x1[s]])
                    for c4 in range(4):
                        ps = next_ps()
                        for c in range(4):
                            cc = c4 * 4 + c
                            tr(ps[:, c * 128:(c + 1) * 128], o_t[:, cc * 128:(cc + 1) * 128], [o_t], [ps])
                        evac(oT[:, c4 * 4:(c4 + 1) * 4, s * 128:(s + 1) * 128], ps[:, :].rearrange("p (j t) -> p j t", t=128), [ps], [(oT, s)])
                for n in range(D // WN):
                    wt = need_w(wk)
                    wk += 1
                    for s in range(ns5):
                        ps = next_ps()
                        for c in range(16):
                            mm(ps[:, 0:WN], oT[:, c, s * 128:(s + 1) * 128], wt[:, c, :], c == 0, c == 15, [(oT, s), wt], [ps])
                        evac(mix[s][:, n * WN:(n + 1) * WN], ps[:, 0:WN], [ps], [mix[s]])
                for s in range(ns5):
                    ss, rs = sss[sk % 4], rss[sk % 4]
                    sk += 1
                    act(junk[:], mix[s][:], AF.Square, [mix[s]], [junk, ss], accum=ss[:])
                    rstd_from_ss(ss, rs, D, 1e-6)
                    ts("dve", mix[s][:], mix[s][:], rs[:], None, ALU.mult, None, [mix[s], rs], [mix[s]])
                    tt("pool", mix[s][:], mix[s][:], gpost[:, 0:D], ALU.mult, [mix[s], gpost], [mix[s]])
                    tt("dve", x1[s][:], mix[s][:], x1[s][:], ALU.add, [mix[s], x1[s]], [x1[s]])
                    ss, rs = sss[sk % 4], rss[sk % 4]
                    sk += 1
                    act(junk[:], x1[s][:], AF.Square, [x1[s]], [junk, ss], accum=ss[:])
                    rstd_from_ss(ss, rs, D, 1e-6)
                    ts("dve", mix[s][:], x1[s][:], rs[:], None, ALU.mult, None, [x1[s], rs], [mix[s]])
                    for c4 in range(4):
                        ps = next_ps()
                        for c in range(4):
                            cc = c4 * 4 + c
                            tr(ps[:, c * 128:(c + 1) * 128], mix[s][:, cc * 128:(cc + 1) * 128], [mix[s]], [ps])
                        for c in range(4):
                            cc = c4 * 4 + c
                            ts("dve", hmT[:, cc, s * 128:(s + 1) * 128], ps[:, c * 128:(c + 1) * 128], gcols[:, 16 + cc:17 + cc], None,
                               ALU.mult, None, [ps, gcols], [(hmT, s)])
                for fg in range(DFF // WN):
                    wt = need_w(wk)
                    wk += 1
                    for fc in range(WN // 128):
                        ps = next_ps()
                        for c in range(16):
                            mm(ps[:, 0:TB5], wt[:, c, fc * 128:(fc + 1) * 128], hmT[:, c, :], c == 0, c == 15, [hmT, wt], [ps])
                        r_t = rl[rk % 2]
                        rk += 1
                        act(r_t[:], ps[:, 0:TB5], AF.Relu, [ps], [r_t])
                        f = fg * (WN // 128) + fc
                        tt("pool", actT[:, f, :], r_t[:], r_t[:], ALU.mult, [r_t], [(actT, f)])
                for n in range(D // WN):
                    accs = [next_ps() for _ in range(ns5)]
                    for f4 in range(4):
                        wt = need_w(wk)
                        wk += 1
                        for s in range(ns5):
                            for c in range(16):
                                f = f4 * 16 + c
                                mm(accs[s][:, 0:WN], actT[:, f, s * 128:(s + 1) * 128], wt[:, c, :], f == 0, f == 63, [(actT, f), wt], [accs[s]])
                    for s in range(ns5):
                        evac(mix[s][:, n * WN:(n + 1) * WN], accs[s][:, 0:WN], [accs[s]], [mix[s]])
                for s in range(ns5):
                    ss, rs = sss[sk % 4], rss[sk % 4]
                    sk += 1
                    act(junk[:], mix[s][:], AF.Square, [mix[s]], [junk, ss], accum=ss[:])
                    rstd_from_ss(ss, rs, D, 1e-6)
                    ts("dve", mix[s][:], mix[s][:], rs[:], None, ALU.mult, None, [mix[s], rs], [mix[s]])
                    tt("pool", mix[s][:], mix[s][:], gpost[:, D:2 * D], ALU.mult, [mix[s], gpost], [mix[s]])
                    tt("dve", mix[s][:], mix[s][:], x1[s][:], ALU.add, [mix[s], x1[s]], [mix[s]])
                    P.dma(y_out[b0 + s * 128:b0 + (s + 1) * 128, :], mix[s][:], reads=[mix[s]], eng="pool")
            P.flush()
        P.stack.close()
    return nc, P


def _consts():
    c = np.zeros((128, 1024), np.float32)
    idx = np.arange(128)
    c[:, 0:128] = np.eye(128, dtype=np.float32)
    for di in range(2):
        before = (idx[:, None] < idx[None, :]) if di == 0 else (idx[:, None] > idx[None, :])
        beq = before | np.eye(128, dtype=bool)
        base = 128 + 384 * di
        c[:, base:base + 128] = before
        c[:, base + 128:base + 256] = beq
        c[:, base + 256:base + 384] = before.T
    c[:, 896:1024] = 1.0
    return c


def prep_shared(inp, rev=False):
    f = lambda a: np.ascontiguousarray(np.asarray(a, dtype=np.float32))
    bc = lambda v: np.broadcast_to(np.asarray(v, np.float32).reshape(1, -1), (128, np.asarray(v).size))
    col = lambda v: np.asarray(v, np.float32).reshape(-1, 128).T
    sh = {}
    perm = np.arange(RW)
    if rev:
        perm[3072:3136], perm[3136:3200] = np.arange(3136, 3200), np.arange(3072, 3136)
        perm[3200:3264], perm[3264:3328] = np.arange(3264, 3328), np.arange(3200, 3264)
    w_in = np.asarray(inp["w_in"][0], np.float32)
    if rev:
        w_in = np.concatenate([w_in[:, :RW][:, perm], w_in[:, RW:]], axis=1)
    sh["w_in"] = f(w_in)
    sh["w_out"] = f(inp["w_out"][0])
    sh["w_up"] = f(inp["w_up"][0])
    sh["w_down"] = f(inp["w_down"][0])
    sh["consts"] = _consts()
    sh["gcols"] = f(np.concatenate([col(inp["g_pre_mix"][0]), col(inp["g_pre_mlp"][0])], axis=1))
    mp, mn = np.asarray(inp["mu_prev"][0], np.float32), np.asarray(inp["mu_next"][0], np.float32)
    if rev:
        mp, mn = mn[perm], mp[perm]
    sh["mu"] = f(np.concatenate([bc(mp), bc(mn)], axis=1))
    sh["vecs"] = f(np.concatenate([bc(inp[k][0]) for k in ("k_k", "k_a", "r_k", "gn_w", "gn_b", "cln_w", "cln_b")], axis=1))
    sh["gpost"] = f(np.concatenate([bc(inp["g_post_mix"][0]), bc(inp["g_post_mlp"][0])], axis=1))
    lora = np.zeros((2, 128, DR), np.float32)
    misc = np.zeros((2, 128, DR), np.float32)
    for di, sfx in enumerate(("b", "f") if rev else ("f", "b")):
        lora[di, 0:64] = inp["w2_" + sfx][0]
        lora[di, 64:128] = inp["a2_" + sfx][0]
        misc[di, 0:32] = inp["g2"][0][128:160]
        misc[di, 32] = inp["w0_" + sfx][0]
        misc[di, 64] = inp["a0_" + sfx][0]
    sh["lora"] = lora
    sh["misc"] = misc
    sh["g2a"] = f(inp["g2"][0][0:128])
    dw = np.asarray(inp["dw_w"][0], np.float32)
    if rev:
        dw = dw[::-1]
    sh["dwT"] = f(dw.T.reshape(8, 128, CW).transpose(1, 0, 2).reshape(128, 8 * CW))
    sh["dwb"] = f(col(inp["dw_b"][0]))
    return sh


_CACHE = {}


def kernel(**inputs):
    inp = {k: np.asarray(v) for k, v in inputs.items()}
    T = inp["x_prompt"].shape[1]
    if T not in _CACHE:
        _CACHE[T] = build(T, TOWN=T // 2)[0]
    nc = _CACHE[T]
    shs = [prep_shared(inp, False), prep_shared(inp, True)]
    seqs = [inp["x_prompt"][0], inp["x_prompt"][1], inp["x_sample"][0]]
    n = 8
    in_maps = []
    for c in range(n):
        cc = c if c < 6 else 0
        rev = cc % 2
        m = dict(shs[rev])
        xs = np.asarray(seqs[cc // 2], dtype=np.float32)
        m["x"] = np.ascontiguousarray(xs[::-1] if rev else xs)
        in_maps.append(m)
    res = run_bass_kernel_spmd(nc, in_maps, core_ids=list(range(n)))
    ys = []
    for sq in range(3):
        a = np.asarray(res.results[2 * sq]["y"], dtype=np.float32)
        b = np.asarray(res.results[2 * sq + 1]["y"], dtype=np.float32)
        ys.append(np.concatenate([a, b[::-1]], axis=0))
    return (np.stack([ys[0], ys[1]], axis=0), ys[2][None])
```

```python
import contextlib
import numpy as np
import concourse.bass as bass
import concourse.mybir as mybir
from concourse.bass_utils import run_bass_kernel_spmd

F32 = mybir.dt.float32
BF16 = mybir.dt.bfloat16
AF = mybir.ActivationFunctionType
ALU = mybir.AluOpType
AX = mybir.AxisListType

D = 2048
DR = 1024
NH = 16
HD = 64
RW = 3488
INC = 5536
DFF = 8192
CW = 31
CDEC = 0.6065306597126334
HIL = 8

ENGS = ("pe", "act", "dve", "pool", "sp")
NDS = 20
EPOCH = 30000


class Buf:
    _n = 0

    psum_ids = set()

    def __init__(self, t, nparts=1, name=None, psum=False):
        self.t = t
        self.nparts = nparts
        Buf._n += 1
        self.id = Buf._n
        self.name = name
        if psum:
            Buf.psum_ids.add(self.id)

    def __getitem__(self, idx):
        return self.t[idx]


def _parts(spec):
    if isinstance(spec, Buf):
        return [(spec.id, p) for p in range(spec.nparts)]
    b, p = spec
    if isinstance(p, int):
        return [(b.id, p)]
    return [(b.id, q) for q in p]


class Prog:
    def __init__(self, nc):
        self.nc = nc
        self.ops = []
        self.last_w = {}
        self.readers = {}
        self.cnt = {e: 0 for e in ENGS}
        self.dcnt = {e: 0 for e in ENGS}
        self.done = []
        self.known = {e: {} for e in ENGS}
        self.sems = {}
        self.flushed = 0
        self.stack = contextlib.ExitStack()
        self.total = {e: 0 for e in ENGS}

    def op(self, eng, fn, reads=(), writes=(), dma=False):
        i = len(self.ops)
        deps = set()
        for r in reads:
            for p in _parts(r):
                w = self.last_w.get(p)
                if w is not None:
                    deps.add((w, False))
                if p[0] in Buf.psum_ids:
                    for rd in self.readers.get(p, ()):
                        deps.add((rd, True))
        for wspec in writes:
            for p in _parts(wspec):
                w = self.last_w.get(p)
                if w is not None:
                    deps.add((w, False))
                for rd in self.readers.get(p, ()):
                    deps.add((rd, True))
        for r in reads:
            for p in _parts(r):
                self.readers.setdefault(p, []).append(i)
        for wspec in writes:
            for p in _parts(wspec):
                self.last_w[p] = i
                self.readers[p] = []
        if dma:
            j = self.dcnt[eng]
            self.dcnt[eng] += 1
            done = (("d", eng, j % NDS), 16 * (j // NDS + 1))
        else:
            j = self.cnt[eng]
            self.cnt[eng] += 1
            done = (("c", eng, j // EPOCH), j % EPOCH + 1)
        self.done.append(done)
        self.ops.append(dict(eng=eng, fn=fn, deps=deps, dma=dma, j=j))
        return i

    def dma(self, out, in_, reads=(), writes=(), eng="sp", **kw):
        return self.op(eng, lambda e: e.dma_start(out=out, in_=in_, **kw), reads, writes, dma=True)

    def sem(self, key):
        s = self.sems.get(key)
        if s is None:
            s = self.stack.enter_context(self.nc.semaphore("s_" + "_".join(str(k) for k in key)))
            self.sems[key] = s
        return s

    def flush(self):
        nc = self.nc
        ops = self.ops
        per_eng = {e: [] for e in ENGS}
        for i in range(self.flushed, len(ops)):
            o = ops[i]
            e = o["eng"]
            need = {}
            for d, war in o["deps"]:
                od = ops[d]
                if od["eng"] == e and not od["dma"]:
                    if e == "pe" or e == "sp" or war:
                        continue
                key, val = self.done[d]
                if need.get(key, 0) < val:
                    need[key] = val
            if o["dma"] and o["j"] >= NDS:
                key = ("d", e, o["j"] % NDS)
                val = 16 * (o["j"] // NDS)
                if need.get(key, 0) < val:
                    need[key] = val
            waits = []
            for key, val in need.items():
                if self.known[e].get(key, 0) >= val:
                    continue
                self.known[e][key] = val
                waits.append((self.sem(key), val))
            key, val = self.done[i]
            per_eng[e].append((waits, o["fn"], self.sem(key), o["dma"]))
            o["fn"] = None
        self.flushed = len(ops)
        handles = dict(pe="tensor", act="scalar", dve="vector", pool="gpsimd", sp="sync")
        dcnt = dict(self.dcnt)
        with nc.Block() as block:
            for e in ENGS:
                lst = per_eng[e]
                if not lst:
                    continue
                self.total[e] += len(lst)

                def body(eh, lst=lst, e0=e):
                    for waits, fn, s, isdma in lst:
                        for ws, wv in waits:
                            eh.wait_ge(ws, wv)
                        fn(eh).then_inc(s, 16 if isdma else 1)
                    for k in range(min(NDS, dcnt[e0])):
                        nuse = (dcnt[e0] - k + NDS - 1) // NDS
                        eh.wait_ge(self.sem(("d", e0, k)), 16 * nuse)
                        self.known[e0][("d", e0, k)] = 16 * nuse

                getattr(block, handles[e])(body)


def build(T, debug=False, nphase=99, cut=99, hcut=99, TOWN=None):
    nc = bass.Bass("TRN2", target_bir_lowering=False)
    P = Prog(nc)
    NCH = T // 128
    TOWN = T if TOWN is None else TOWN
    NOWN = TOWN // 128

    def din(name, shape):
        return nc.dram_tensor(name, list(shape), F32, kind="ExternalInput").ap()

    def dscr(name, shape):
        return nc.dram_tensor(name, list(shape), F32, kind="ExternalOutput" if debug else "Internal").ap()

    x = din("x", [T, D])
    w_in = din("w_in", [D, INC])
    w_out = din("w_out", [D, D])
    w_up = din("w_up", [D, DFF])
    w_down = din("w_down", [DFF, D])
    consts_d = din("consts", [128, 1024])
    gcols_d = din("gcols", [128, 32])
    mu_d = din("mu", [128, 2 * RW])
    vecs_d = din("vecs", [128, 7 * DR])
    gpost_d = din("gpost", [128, 2 * D])
    lora_d = din("lora", [2, 128, DR])
    misc_d = din("misc", [2, 128, DR])
    g2a_d = din("g2a", [128, DR])
    dwT_d = din("dwT", [128, 8 * CW])
    dwb_d = din("dwb", [128, 8])
    y_out = nc.dram_tensor("y", [TOWN, D], F32, kind="ExternalOutput").ap()

    Z = dscr("Zs", [T + 2, INC])
    YF = dscr("YFs", [T, DR])
    O = dscr("Os", [T, D])
    UT = dscr("UTs", [DR, T + 30])
    Zb, YFb, Ob, UTb = Buf(Z, NCH + 2), Buf(YF, NCH), Buf(O, 2 * NCH), Buf(UT, 1)
    wbf = {}
    for nm, src in (("w_in", w_in), ("w_out", w_out), ("w_up", w_up), ("w_down", w_down)):
        t = nc.dram_tensor(nm + "_bf", list(src.shape), BF16, kind="Internal").ap()
        wbf[nm] = (t, Buf(t, 1, nm + "_bf"), src)

    gst = contextlib.ExitStack()
    with gst, nc.allow_low_precision("bf16 matmul operands, fp32 PSUM accumulation"):
        def sbg(name, shape, nparts=1):
            return Buf(gst.enter_context(nc.sbuf_tensor("sb_" + name, list(shape), F32)), nparts, name)

        banks = [Buf(gst.enter_context(nc.psum_tensor("bank%d" % i, [128, 512], F32)), 1, "bank%d" % i, psum=True) for i in range(8)]
        bstate = [0]

        def next_ps():
            b = banks[bstate[0] % 8]
            bstate[0] += 1
            return b

        for nm in ("w_in", "w_out", "w_up", "w_down"):
            t, tb, src = wbf[nm]
            rows = src.shape[0]
            step = 128 if src.shape[1] > 4096 else 256
            for r0 in range(0, rows, step):
                P.dma(t[r0:r0 + step, :], src[r0:r0 + step, :], writes=[tb], eng="pool")
        consts = sbg("consts", [128, 1024])
        gcols = sbg("gcols", [128, 32])
        P.dma(consts[:], consts_d, writes=[consts])
        P.dma(gcols[:], gcols_d, writes=[gcols])
        ident = consts[:, 0:128]
        ones = consts.t[:, 896:1024]

        def mask2(di):
            return consts[:, 128 + 384 * di: 384 + 384 * di]

        def maskT(di):
            return consts[:, 384 + 384 * di: 512 + 384 * di]

        def tri_le(di):
            return consts[:, 256 + 384 * di: 384 + 384 * di]

        ecnt = [0]

        def evac(out, in_, reads, writes, eng=None):
            if eng is None:
                eng = "act" if ecnt[0] % 2 == 0 else "dve"
                ecnt[0] += 1
            if eng == "act":
                P.op("act", lambda e: e.copy(out=out, in_=in_), reads, writes)
            else:
                P.op(eng, lambda e: e.tensor_copy(out=out, in_=in_), reads, writes)

        def tt(eng, out, in0, in1, op, reads, writes):
            P.op(eng, lambda e: e.tensor_tensor(out=out, in0=in0, in1=in1, op=op), reads, writes)

        def ts(eng, out, in0, s1, s2, op0, op1, reads, writes):
            if op1 is None:
                P.op(eng, lambda e: e.tensor_scalar(out=out, in0=in0, scalar1=s1, scalar2=None, op0=op0), reads, writes)
            else:
                P.op(eng, lambda e: e.tensor_scalar(out=out, in0=in0, scalar1=s1, scalar2=s2, op0=op0, op1=op1), reads, writes)

        def stt(out, in0, scalar, in1, op0, op1, reads, writes):
            P.op("dve", lambda e: e.scalar_tensor_tensor(out=out, in0=in0, scalar=scalar, in1=in1, op0=op0, op1=op1), reads, writes)

        def act(out, in_, func, reads, writes, scale=None, accum=None):
            kw = {}
            if scale is not None:
                kw["scale"] = scale
            if accum is not None:
                kw["accum_out"] = accum
            P.op("act", lambda e: e.activation(out=out, in_=in_, func=func, **kw), reads, writes)

        def mm(out, lhsT, rhs, start, stop, reads, writes):
            P.op("pe", lambda e: e.matmul(out, lhsT=lhsT, rhs=rhs, start=start, stop=stop), reads, writes)

        def tr(out, in_, reads, writes):
            P.op("pe", lambda e: e.transpose(out=out, in_=in_, identity=ident), list(reads) + [consts], writes)

        def rstd_from_ss(ss, rs, n, eps, reads_extra=()):
            ts("dve", rs[:], ss[:], 1.0 / n, eps, ALU.mult, ALU.add, [ss], [rs])
            act(rs[:], rs[:], AF.Sqrt, [rs], [rs])
            P.op("dve", lambda e: e.reciprocal(out=rs[:], in_=rs[:]), [rs], [rs])

        TB = min(512, T)
        nsub = TB // 128
        with contextlib.ExitStack() as st:
            def sb(name, shape, nparts=1, dt=F32):
                return Buf(st.enter_context(nc.sbuf_tensor("sb_" + name, list(shape), dt)), nparts, name)
            xb = [sb("p1x%d" % i, [128, D]) for i in range(2)]
            junk = sb("p1junk", [128, D])
            ssb = [sb("p1ss%d" % i, [128, 1]) for i in range(2)]
            rsb = [sb("p1rs%d" % i, [128, 1]) for i in range(2)]
            hT = sb("p1hT", [128, 16, TB], nsub, BF16)
            wb = [sb("p1w%d" % i, [128, 16, 512], 1, BF16) for i in range(3)]
            wlist = []
            for _b in range(T // TB):
                _n0 = 0
                while _n0 < INC:
                    _nw = min(512, INC - _n0)
                    wlist.append((_n0, _nw))
                    _n0 += _nw
            issued = [0]

            def need_w(i):
                while issued[0] <= min(i + 2, len(wlist) - 1):
                    j = issued[0]
                    a, b = wlist[j]
                    P.dma(wb[j % 3][:, :, 0:b], wbf["w_in"][0][:, a:a + b].rearrange("(c p) n -> p c n", p=128), reads=[wbf["w_in"][1]], writes=[wb[j % 3]])
                    issued[0] += 1
                return wb[i % 3]
            zsb = [sb("p1z%d" % i, [128, 512]) for i in range(2)]
            zero = sb("p1zero", [1, 1384])
            P.op("pool", lambda e: e.memset(zero[:], 0.0), [], [zero])
            for q4 in range(4):
                P.dma(Z[0:1, q4 * 1384:(q4 + 1) * 1384], zero[:], reads=[zero], writes=[(Zb, 0)])
                P.dma(Z[T + 1:T + 2, q4 * 1384:(q4 + 1) * 1384], zero[:], reads=[zero], writes=[(Zb, NCH + 1)])
            wk = 0
            zk = 0
            xk = 0
            for blk in range(T // TB):
                for s in range(nsub):
                    xt, ss, rs = xb[xk % 2], ssb[xk % 2], rsb[xk % 2]
                    xk += 1
                    r0 = blk * TB + s * 128
                    P.dma(xt[:], x[r0:r0 + 128, :], writes=[xt])
                    act(junk[:], xt[:], AF.Square, [xt], [junk, ss], accum=ss[:])
                    rstd_from_ss(ss, rs, D, 1e-6)
                    ts("dve", xt[:], xt[:], rs[:], None, ALU.mult, None, [xt, rs], [xt])
                    for c4 in range(4):
                        ps = next_ps()
                        for c in range(4):
                            tr(ps[:, c * 128:(c + 1) * 128], xt[:, (c4 * 4 + c) * 128:(c4 * 4 + c + 1) * 128], [xt], [ps])
                        for c in range(4):
                            cc = c4 * 4 + c
                            ts("dve", hT[:, cc, s * 128:(s + 1) * 128], ps[:, c * 128:(c + 1) * 128], gcols[:, cc:cc + 1], None,
                               ALU.mult, None, [ps, gcols], [(hT, s)])
                n0 = 0
                while n0 < INC:
                    nw = min(512, INC - n0)
                    wt = need_w(wk)
                    wk += 1
                    for s in range(nsub):
                        ps = next_ps()
                        for c in range(16):
                            mm(ps[:, 0:nw], hT[:, c, s * 128:(s + 1) * 128], wt[:, c, 0:nw], c == 0, c == 15, [(hT, s), wt], [ps])
                        zt = zsb[zk % 2]
                        zk += 1
                        evac(zt[:, 0:nw], ps[:, 0:nw], [ps], [zt])
                        ch = (blk * TB + s * 128) // 128
                        r0 = 1 + blk * TB + s * 128
                        P.dma(Z[r0:r0 + 128, n0:n0 + nw], zt[:, 0:nw], reads=[zt], writes=[(Zb, 1 + ch)], eng="pool")
                    n0 += nw
            P.flush()

        for di in range(2 if nphase >= 3 else (1 if nphase >= 2 else 0)):
            with contextlib.ExitStack() as st:
                def sb(name, shape, nparts=1, dt=F32):
                    return Buf(st.enter_context(nc.sbuf_tensor("sb_" + name, list(shape), dt)), nparts, name)
                pf = "s%d" % di
                mu = sb(pf + "mu", [128, 2 * RW])
                vecs = sb(pf + "vecs", [128, 5 * DR])
                lora = sb(pf + "lora", [128, DR])
                misc = sb(pf + "misc", [128, DR])
                g2a = sb(pf + "g2a", [128, DR])
                P.dma(mu[:], mu_d, writes=[mu])
                P.dma(vecs[:], vecs_d[:, 0:5 * DR], writes=[vecs])
                P.dma(lora[:], lora_d[di], writes=[lora])
                P.dma(misc[:], misc_d[di], writes=[misc])
                P.dma(g2a[:], g2a_d, writes=[g2a])
                kk_b, ka_b, rk_b, gnw_b, gnb_b = [vecs[:, i * DR:(i + 1) * DR] for i in range(5)]
                stg = [sb(pf + "stg%d" % i, [128, DR]) for i in range(3)]
                zsl = [sb(pf + "zs%d" % i, [128, RW]) for i in range(2)]
                tl = [sb(pf + "t%d" % i, [128, DR]) for i in range(8)]
                t_sig, t_a, t_kk, t_km, t_b, t_e, t_e2 = tl[:7]
                Yt = Buf(tl[7].t, NH, pf + "Yt")
                t_At, t_Bt, t_Kt, t_Rt, t_Bp, t_Kp, t_vb = [sb(pf + "tb%d" % i, [128, DR], 1, BF16) for i in range(7)]
                identb = sb(pf + "identb", [128, 128], 1, BF16)
                P.op("pool", lambda e, identb=identb: e.tensor_copy(out=identb[:], in_=ident), [consts], [identb])
                LT = sb(pf + "LT", [128, 128])
                LTT = sb(pf + "LTT", [128, 128])
                sm = sb(pf + "sm", [128, 64])
                Wc = sb(pf + "Wc", [64, 16])
                SG = sb(pf + "SG", [128, 160])
                sgT = sb(pf + "sgT", [128, 256])
                ST = sb(pf + "ST", [64, DR], NH, BF16)
                featT = [sb(pf + "fT%d" % i, [128, 512], 1, BF16) for i in range(max(2, HIL // 2))]
                MAs = [sb(pf + "MA%d" % i, [128, 256], 1, BF16) for i in range(HIL)]
                MBs = [sb(pf + "MB%d" % i, [128, 256], 1, BF16) for i in range(HIL)]
                PTa = [sb(pf + "PTa%d" % i, [128, 128], 1, BF16) for i in range(HIL)]
                PP = [[sb(pf + "PP%d_%d" % (i, j), [128, 256], 1, BF16) for j in range(2)] for i in range(HIL)]
                ZZ = [[sb(pf + "ZZ%d_%d" % (i, j), [128, 128], 1, BF16) for j in range(2)] for i in range(HIL)]
                GTs = [sb(pf + "GT%d" % i, [64, 64], 1, BF16) for i in range(HIL)]
                RhTs = [sb(pf + "RhT%d" % i, [64, 128], 1, BF16) for i in range(HIL)]
                P.op("pool", lambda e, ST=ST: e.memset(ST[:], 0.0), [], [ST])
                if debug:
                    print("SBUF remaining in sweep", di, nc.sbuf_bytes_remaining)
                m2, mT, tle = mask2(di), maskT(di), tri_le(di)
                blocks = [(0, 1024), (1024, 2048), (2048, 3072), (3072, RW)]
                order = range(NOWN) if di == 0 else range(NCH - 1, -1, -1)
                order = list(order)

                def mix_gen(ch, zs):
                    t0 = ch * 128
                    for (c0, c1) in blocks:
                        n = c1 - c0
                        zc, zp, zn = stg
                        P.dma(zc[:, 0:n], Z[1 + t0:1 + t0 + 128, c0:c1], reads=[(Zb, 1 + ch)], writes=[zc])
                        P.dma(zp[:, 0:n], Z[t0:t0 + 128, c0:c1], reads=[(Zb, [ch, 1 + ch])], writes=[zp])
                        P.dma(zn[:, 0:n], Z[2 + t0:2 + t0 + 128, c0:c1], reads=[(Zb, [1 + ch, 2 + ch])], writes=[zn])
                        yield
                        tt("pool", zp[:, 0:n], zp[:, 0:n], zc[:, 0:n], ALU.subtract, [zp, zc], [zp])
                        tt("dve", zn[:, 0:n], zn[:, 0:n], zc[:, 0:n], ALU.subtract, [zn, zc], [zn])
                        yield
                        tt("pool", zp[:, 0:n], zp[:, 0:n], mu[:, c0:c1], ALU.mult, [zp, mu], [zp])
                        tt("dve", zn[:, 0:n], zn[:, 0:n], mu[:, RW + c0:RW + c1], ALU.mult, [zn, mu], [zn])
                        yield
                        tt("pool", zs[:, c0:c1], zc[:, 0:n], zp[:, 0:n], ALU.add, [zc, zp], [zs])
                        tt("dve", zs[:, c0:c1], zs[:, c0:c1], zn[:, 0:n], ALU.add, [zs, zn], [zs])
                        yield

                for idx, ch in enumerate(order):
                    t0 = ch * 128
                    own = ch < NOWN
                    zs = zsl[idx % 2]
                    if idx == 0:
                        for _ in mix_gen(ch, zs):
                            pass
                    nxt_mix = mix_gen(order[idx + 1], zsl[(idx + 1) % 2]) if idx + 1 < len(order) else None
                    r_, k_, v_ = zs[:, 0:DR], zs[:, DR:2 * DR], zs[:, 2 * DR:3 * DR]
                    xw = zs[:, 3072 + 64 * di:3136 + 64 * di]
                    xa = zs[:, 3200 + 64 * di:3264 + 64 * di]
                    xg = zs[:, 3328:3488]
                    if cut <= 1:
                        continue
                    act(LT[:, 0:64], xw, AF.Tanh, [zs], [LT])
                    P.op("act", lambda e, xa=xa: e.copy(out=LT[:, 64:128], in_=xa), [zs], [LT])
                    ps = next_ps()
                    tr(ps[:, 0:128], LT[:], [LT], [ps])
                    evac(LTT[:], ps[:, 0:128], [ps], [LTT])
                    for half in range(2):
                        hsl = slice(half * 512, (half + 1) * 512)
                        ps = next_ps()
                        mm(ps[:, :], LTT[0:64, :], lora[0:64, hsl], True, False, [LTT, lora], [ps])
                        mm(ps[:, :], ones[32:33, :], misc[32:33, hsl], False, True, [consts, misc], [ps])
                        act(t_sig[:, hsl], ps[:, :], AF.Sigmoid, [ps], [t_sig])
                        ps = next_ps()
                        mm(ps[:, :], LTT[64:128, :], lora[64:128, hsl], True, False, [LTT, lora], [ps])
                        mm(ps[:, :], ones[64:65, :], misc[64:65, hsl], False, True, [consts, misc], [ps])
                        act(t_a[:, hsl], ps[:, :], AF.Sigmoid, [ps], [t_a])
                    if cut <= 2:
                        continue
                    tt("dve", t_kk[:], k_, kk_b, ALU.mult, [zs, vecs], [t_kk])
                    tt("dve", t_e[:], t_kk[:], t_kk[:], ALU.mult, [t_kk], [t_e])
                    P.op("dve", lambda e: e.tensor_reduce(out=sm[:, 0:16], in_=t_e[:].rearrange("p (h k) -> p h k", k=HD), axis=AX.X, op=ALU.add), [t_e], [sm])
                    act(sm[:, 0:16], sm[:, 0:16], AF.Sqrt, [sm], [sm])
                    ts("dve", sm[:, 0:16], sm[:, 0:16], 1e-12, None, ALU.max, None, [sm], [sm])
                    P.op("dve", lambda e: e.reciprocal(out=sm[:, 0:16], in_=sm[:, 0:16]), [sm], [sm])
                    tt("dve", t_kk[:].rearrange("p (h k) -> p h k", k=HD), t_kk[:].rearrange("p (h k) -> p h k", k=HD),
                       sm[:, 0:16].unsqueeze(2).to_broadcast([128, NH, HD]), ALU.mult, [t_kk, sm], [t_kk])
                    if cut <= 3:
                        continue
                    stt(t_km[:], t_a[:], -1.0, ka_b, ALU.add, ALU.mult, [t_a, vecs], [t_km])
                    stt(t_km[:], t_km[:], 1.0, k_, ALU.add, ALU.mult, [t_km, zs], [t_km])
                    tt("dve", t_b[:], t_kk[:], t_a[:], ALU.mult, [t_kk, t_a], [t_b])
                    if cut <= 4:
                        continue
                    for half in range(2):
                        hsl = slice(half * 512, (half + 1) * 512)
                        ps = next_ps()
                        mm(ps[:, :], tle, t_sig[:, hsl], True, True, [consts, t_sig], [ps])
                        act(t_e[:, hsl], ps[:, :], AF.Exp, [ps], [t_e], scale=-CDEC)
                        tt("dve", t_Rt[:, hsl], r_[:, hsl], t_e[:, hsl], ALU.mult, [zs, t_e], [t_Rt])
                        act(t_e[:, hsl], ps[:, :], AF.Exp, [ps, t_Rt], [t_e], scale=CDEC)
                        tt("dve", t_Bt[:, hsl], t_b[:, hsl], t_e[:, hsl], ALU.mult, [t_b, t_e], [t_Bt])
                        tt("dve", t_Kt[:, hsl], t_km[:, hsl], t_e[:, hsl], ALU.mult, [t_km, t_e], [t_Kt])
                        tt("dve", t_e2[:, hsl], ps[:, :], t_sig[:, hsl], ALU.subtract, [ps, t_sig], [t_e2])
                        act(t_e[:, hsl], t_e2[:, hsl], AF.Exp, [t_e2, t_Bt, t_Kt], [t_e], scale=-CDEC)
                        stt(t_At[:, hsl], t_kk[:, hsl], -1.0, t_e[:, hsl], ALU.mult, ALU.mult, [t_kk, t_e], [t_At])
                        ps = next_ps()
                        mm(ps[:, :], mT, t_sig[:, hsl], True, True, [consts, t_sig], [ps])
                        act(t_e[:, hsl], ps[:, :], AF.Exp, [ps, t_At], [t_e], scale=-CDEC)
                        tt("dve", t_Bp[:, hsl], t_b[:, hsl], t_e[:, hsl], ALU.mult, [t_b, t_e], [t_Bp])
                        tt("dve", t_Kp[:, hsl], t_km[:, hsl], t_e[:, hsl], ALU.mult, [t_km, t_e], [t_Kp])
                    if cut <= 5:
                        continue
                    P.op("pool", lambda e, v_=v_: e.tensor_copy(out=t_vb[:], in_=v_), [zs], [t_vb])
                    ps = next_ps()
                    for h in range(NH):
                        mm(ps[0:64, h:h + 1], t_sig[:, h * 64:(h + 1) * 64], ones[:, 0:1], True, True, [t_sig, consts], [ps])
                    act(Wc[:], ps[0:64, 0:16], AF.Exp, [ps], [Wc], scale=-CDEC)
                    if di == 1 and own:
                        act(SG[:], xg, AF.Sigmoid, [zs], [SG])
                        ps = next_ps()
                        tr(ps[:, 0:128], SG[:, 0:128], [SG], [ps])
                        tr(ps[0:32, 128:256], SG[:, 128:160], [SG], [ps])
                        evac(sgT[:], ps[:, 0:256], [ps], [sgT])
                        tt("dve", t_e[:], r_, k_, ALU.mult, [zs, t_Kp, t_Bp], [t_e])
                        tt("dve", t_e[:], t_e[:], rk_b, ALU.mult, [t_e, vecs], [t_e])
                        P.op("dve", lambda e: e.tensor_reduce(out=sm[:, 16:32], in_=t_e[:].rearrange("p (h k) -> p h k", k=HD), axis=AX.X, op=ALU.add), [t_e], [sm])
                    if cut <= 6:
                        continue
                    pend = []
                    for hp in range(NH // 2):
                        cs = slice(hp * 128, (hp + 1) * 128)
                        fT = featT[hp % max(2, HIL // 2)]
                        ps = next_ps()
                        psb = ps[:, 0:256].bitcast(BF16)
                        for j, src in enumerate((t_At, t_Rt, t_Bt, t_Kt)):
                            P.op("pe", lambda e, j=j, src=src, psb=psb, cs=cs: e.transpose(out=psb[:, j * 128:(j + 1) * 128], in_=src[:, cs], identity=identb[:]), [src, identb], [ps])
                        evac(fT[:], psb, [ps], [fT])
                        if hcut <= 1:
                            continue
                        def head_gen(hh, hp=hp, fT=fT):
                            h = 2 * hp + hh
                            pb = 64 * hh
                            hs = slice(h * 64, (h + 1) * 64)
                            q = h % HIL
                            MA, MB, PT0, GT, RhT = MAs[q], MBs[q], PTa[q], GTs[q], RhTs[q]
                            AtT = fT[pb:pb + 64, 0:128]
                            BtT = fT[pb:pb + 64, 256:384]
                            KtT = fT[pb:pb + 64, 384:512]
                            AR = fT[pb:pb + 64, 0:256]
                            vh = t_vb[:, hs]
                            psA = next_ps()
                            mm(psA[:, 0:256], KtT, AR, True, True, [fT], [psA])
                            psB = next_ps()
                            mm(psB[:, 0:256], BtT, AR, True, True, [fT], [psB])
                            mm(psB[:, 256:384], AtT, BtT, True, True, [fT], [psB])
                            if hcut <= 2:
                                return
                            tt("dve", MA[:], psA[:, 0:256], m2, ALU.mult, [psA, consts], [MA])
                            tt("dve", MB[:], psB[:, 0:256], m2, ALU.mult, [psB, consts], [MB])
                            tt("dve", PT0[:], psB[:, 256:384], mT, ALU.mult, [psB, consts], [PT0])
                            yield
                            if hcut <= 3:
                                return
                            psZ = next_ps()
                            mm(psZ[:, 0:64], MA[:, 0:128], vh, True, True, [MA, t_vb], [psZ])
                            Zc = ZZ[q][0]
                            P.op("pool", lambda e, Zc=Zc, hs=hs: e.tensor_copy(out=Zc[:, 0:64], in_=t_At[:, hs]), [t_At], [Zc])
                            evac(Zc[:, 64:128], psZ[:, 0:64], [psZ], [Zc], eng="act")
                            yield
                            if hcut <= 4:
                                return
                            Pc, PTc = MB[:, 0:128], PT0[:]
                            Pbuf, PTbuf = MB, PT0
                            for i in range(7):
                                psz = next_ps()
                                mm(psz[:, 0:128], Pc, Zc[:], True, True, [Pbuf, Zc], [psz])
                                Zn = ZZ[q][(i + 1) % 2]
                                tt("dve", Zn[:], Zc[:], psz[:, 0:128], ALU.add, [Zc, psz], [Zn])
                                Zc = Zn
                                yield
                                if i < 6:
                                    psp = next_ps()
                                    mm(psp[:, 0:128], PTc, Pc, True, True, [Pbuf, PTbuf], [psp])
                                    nb = PP[q][i % 2]
                                    if i < 5:
                                        mm(psp[:, 128:256], Pc, PTc, True, True, [Pbuf, PTbuf], [psp])
                                        evac(nb[:, 0:256], psp[:, 0:256], [psp], [nb], eng="act")
                                    else:
                                        evac(nb[:, 0:128], psp[:, 0:128], [psp], [nb], eng="act")
                                    Pc, PTc = nb[:, 0:128], nb[:, 128:256]
                                    Pbuf = PTbuf = nb
                                    yield
                            if hcut <= 5:
                                return
                            Ah, U0 = Zc[:, 0:64], Zc[:, 64:128]
                            psG = next_ps()
                            mm(psG[0:64, 0:64], Ah, t_Bp[:, hs], True, True, [Zc, t_Bp], [psG])
                            if hcut == 51:
                                return
                            if own:
                                psR = next_ps()
                                mm(psR[0:64, 128:256], Ah, MB[:, 128:256], True, False, [Zc, MB], [psR])
                                mm(psR[0:64, 128:256], t_Rt[:, hs], identb[:], False, True, [t_Rt, identb], [psR])
                            if hcut == 52:
                                return
                            stt(GT[:], consts[0:64, 0:64], Wc[:, h:h + 1], psG[0:64, 0:64], ALU.mult, ALU.add, [consts, Wc, psG], [GT])
                            if hcut == 53:
                                return
                            if own:
                                evac(RhT[:], psR[0:64, 128:256], [psR], [RhT], eng="act")
                            yield
                            if hcut <= 6:
                                return
                            if own:
                                psY = next_ps()
                                mm(psY[:, 0:64], MB[:, 128:256], U0, True, False, [MB, Zc], [psY])
                                mm(psY[:, 0:64], MA[:, 128:256], vh, False, False, [MA, t_vb], [psY])
                                mm(psY[:, 0:64], RhT[:], ST[0:64, hs], False, True, [RhT, (ST, h)], [psY])
                            if hcut <= 7:
                                return
                            psS = next_ps()
                            mm(psS[0:64, 128:192], GT[:], ST[0:64, hs], True, False, [GT, (ST, h)], [psS])
                            mm(psS[0:64, 128:192], t_Kp[:, hs], vh, False, False, [t_Kp, t_vb], [psS])
                            mm(psS[0:64, 128:192], t_Bp[:, hs], U0, False, True, [t_Bp, Zc], [psS])
                            if own:
                                evac(Yt[:, hs], psY[:, 0:64], [psY], [(Yt, h)], eng="act")
                            evac(ST[0:64, hs], psS[0:64, 128:192], [psS], [(ST, h)], eng="dve")
                        pend.append(head_gen(0))
                        pend.append(head_gen(1))
                        if len(pend) < HIL and hp != NH // 2 - 1:
                            continue
                        gens = pend
                        pend = []
                        if nxt_mix is not None:
                            gens.append(nxt_mix)
                            nxt_mix = None
                        while gens:
                            for g_ in list(gens):
                                try:
                                    next(g_)
                                except StopIteration:
                                    gens.remove(g_)
                    if nxt_mix is not None:
                        for _ in nxt_mix:
                            pass
                    if cut <= 8:
                        continue
                    if not own:
                        continue
                    if di == 0:
                        P.dma(YF[t0:t0 + 128, :], Yt[:], reads=[Yt], writes=[(YFb, ch)], eng="pool")
                    else:
                        t_yf, t_g, t_y, t_sq = t_sig, t_a, t_kk, t_km
                        P.dma(t_yf[:], YF[t0:t0 + 128, :], reads=[(YFb, ch)], writes=[t_yf])
                        for half in range(2):
                            hsl = slice(half * 512, (half + 1) * 512)
                            ps = next_ps()
                            mm(ps[:, :], sgT[:, 0:128], g2a[:, hsl], True, False, [sgT, g2a], [ps])
                            mm(ps[:, :], sgT[0:32, 128:256], misc[0:32, hsl], False, True, [sgT, misc], [ps])
                            evac(t_g[:, hsl], ps[:, :], [ps], [t_g], eng="act")
                        tt("dve", t_y[:], Yt[:], t_yf[:], ALU.add, [Yt, t_yf], [t_y])
                        v3 = lambda t: t[:].rearrange("p (h k) -> p h k", k=HD)
                        bc = lambda a: a.unsqueeze(2).to_broadcast([128, NH, HD])
                        P.op("dve", lambda e: e.tensor_reduce(out=sm[:, 32:48], in_=v3(t_y), axis=AX.X, op=ALU.add), [t_y], [sm])
                        ts("dve", sm[:, 32:48], sm[:, 32:48], 1.0 / HD, None, ALU.mult, None, [sm], [sm])
                        tt("dve", v3(t_y), v3(t_y), bc(sm[:, 32:48]), ALU.subtract, [t_y, sm], [t_y])
                        tt("dve", t_sq[:], t_y[:], t_y[:], ALU.mult, [t_y], [t_sq])
                        P.op("dve", lambda e: e.tensor_reduce(out=sm[:, 48:64], in_=v3(t_sq), axis=AX.X, op=ALU.add), [t_sq], [sm])
                        ts("dve", sm[:, 48:64], sm[:, 48:64], 1.0 / HD, 64e-5, ALU.mult, ALU.add, [sm], [sm])
                        act(sm[:, 48:64], sm[:, 48:64], AF.Sqrt, [sm], [sm])
                        P.op("dve", lambda e: e.reciprocal(out=sm[:, 48:64], in_=sm[:, 48:64]), [sm], [sm])
                        tt("dve", v3(t_y), v3(t_y), bc(sm[:, 48:64]), ALU.mult, [t_y, sm], [t_y])
                        tt("dve", t_y[:], t_y[:], gnw_b, ALU.mult, [t_y, vecs], [t_y])
                        tt("dve", t_y[:], t_y[:], gnb_b, ALU.add, [t_y, vecs], [t_y])
                        tt("dve", v3(t_sq), zs[:, 2 * DR:3 * DR].rearrange("p (h k) -> p h k", k=HD), bc(sm[:, 16:32]), ALU.mult, [zs, sm, t_sq], [t_sq])
                        tt("dve", t_y[:], t_y[:], t_sq[:], ALU.add, [t_y, t_sq], [t_y])
                        tt("dve", t_y[:], t_y[:], t_g[:], ALU.mult, [t_y, t_g], [t_y])
                        P.dma(O[t0:t0 + 128, 0:DR], t_y[:], reads=[t_y], writes=[(Ob, 2 * ch)], eng="pool")
                P.flush()

        for _ in range(1 if nphase >= 4 else 0):
          with contextlib.ExitStack() as st:
            def sb(name, shape, nparts=1):
                return Buf(st.enter_context(nc.sbuf_tensor("sb_" + name, list(shape), F32)), nparts, name)
            zcv = [sb("c_zcv%d" % i, [128, 2048]) for i in range(2)]
            uu = [sb("c_u%d" % i, [128, DR]) for i in range(2)]
            uT = [sb("c_uT%d" % i, [128, 8, 128]) for i in range(2)]
            zpad = sb("c_zpad", [128, 8, 15])
            P.op("pool", lambda e: e.memset(zpad[:], 0.0), [], [zpad])
            P.dma(UT[:, 0:15].rearrange("(j p) t -> p j t", p=128), zpad[:], reads=[zpad], writes=[UTb])
            P.dma(UT[:, T + 15:T + 30].rearrange("(j p) t -> p j t", p=128), zpad[:], reads=[zpad], writes=[UTb])
            for ch in range(min(NCH, NOWN + 1)):
                t0 = ch * 128
                zt, u, uTt = zcv[ch % 2], uu[ch % 2], uT[ch % 2]
                P.dma(zt[:], Z[1 + t0:1 + t0 + 128, RW:INC], reads=[(Zb, 1 + ch)], writes=[zt])
                act(zt[:, DR:2 * DR], zt[:, DR:2 * DR], AF.Sigmoid, [zt], [zt])
                tt("dve", u[:], zt[:, 0:DR], zt[:, DR:2 * DR], ALU.mult, [zt], [u])
                for j4 in range(2):
                    ps = next_ps()
                    for j in range(4):
                        jj = j4 * 4 + j
                        tr(ps[:, j * 128:(j + 1) * 128], u[:, jj * 128:(jj + 1) * 128], [u], [ps])
                    evac(uTt[:, j4 * 4:(j4 + 1) * 4, :], ps[:, :].rearrange("p (j t) -> p j t", t=128), [ps], [uTt])
                P.dma(UT[:, 15 + t0:15 + t0 + 128].rearrange("(j p) t -> p j t", p=128), uTt[:], reads=[uTt], writes=[UTb], eng="pool")
            P.flush()
        for _ in range(1 if nphase >= 5 else 0):
          with contextlib.ExitStack() as st:
            def sb(name, shape, nparts=1):
                return Buf(st.enter_context(nc.sbuf_tensor("sb_" + name, list(shape), F32)), nparts, name)
            TBc = min(512, TOWN)
            nsc = TBc // 128
            dwT = sb("c_dwT", [128, 8 * CW])
            dwb = sb("c_dwb", [128, 8])
            cvec = sb("c_vec", [128, 2 * DR])
            P.dma(dwT[:], dwT_d, writes=[dwT])
            P.dma(dwb[:], dwb_d, writes=[dwb])
            P.dma(cvec[:], vecs_d[:, 5 * DR:7 * DR], writes=[cvec])
            ub = [sb("c_ub%d" % i, [128, TBc + 30]) for i in range(2)]
            acc = [sb("c_acc%d" % i, [128, TBc]) for i in range(2)]
            ctok = [sb("c_ct%d" % i, [128, DR]) for i in range(nsc)]
            stats = sb("c_st", [128, 12])
            mv = sb("c_mv", [128, 2])
            rsd = sb("c_rs", [128, 1])
            k2 = 0
            for blk in range(TOWN // TBc):
                b0 = blk * TBc
                for j in range(8):
                    ut, ac = ub[k2 % 2], acc[k2 % 2]
                    k2 += 1
                    P.dma(ut[:], UT[j * 128:(j + 1) * 128, b0:b0 + TBc + 30], reads=[UTb], writes=[ut])
                    ts("dve", ac[:], ut[:, 0:TBc], dwT[:, j * CW:j * CW + 1], dwb[:, j:j + 1], ALU.mult, ALU.add, [ut, dwT, dwb], [ac])
                    for tap in range(1, CW):
                        stt(ac[:], ut[:, tap:tap + TBc], dwT[:, j * CW + tap:j * CW + tap + 1], ac[:], ALU.mult, ALU.add, [ut, dwT, ac], [ac])
                    ps = next_ps()
                    for s in range(nsc):
                        tr(ps[:, s * 128:(s + 1) * 128], ac[:, s * 128:(s + 1) * 128], [ac], [ps])
                    for s in range(nsc):
                        evac(ctok[s][:, j * 128:(j + 1) * 128], ps[:, s * 128:(s + 1) * 128], [ps], [ctok[s]], eng="act")
                for s in range(nsc):
                    ct = ctok[s]
                    ch = (b0 + s * 128) // 128
                    for hf in range(2):
                        P.op("dve", lambda e, ct=ct, hf=hf: e.bn_stats(out=stats[:, hf * 6:(hf + 1) * 6], in_=ct[:, hf * 512:(hf + 1) * 512]), [ct], [stats])
                    P.op("dve", lambda e: e.bn_aggr(out=mv[:], in_=stats[:]), [stats], [mv])
                    ts("dve", rsd[:], mv[:, 1:2], 1e-5, None, ALU.add, None, [mv], [rsd])
                    act(rsd[:], rsd[:], AF.Sqrt, [rsd], [rsd])
                    P.op("dve", lambda e: e.reciprocal(out=rsd[:], in_=rsd[:]), [rsd], [rsd])
                    ts("dve", ct[:], ct[:], mv[:, 0:1], rsd[:], ALU.subtract, ALU.mult, [ct, mv, rsd], [ct])
                    tt("pool", ct[:], ct[:], cvec[:, 0:DR], ALU.mult, [ct, cvec], [ct])
                    tt("pool", ct[:], ct[:], cvec[:, DR:2 * DR], ALU.add, [ct, cvec], [ct])
                    act(ct[:], ct[:], AF.Silu, [ct], [ct])
                    P.dma(O[b0 + s * 128:b0 + (s + 1) * 128, DR:2 * DR], ct[:], reads=[ct], writes=[(Ob, 2 * ch + 1)], eng="pool")
            P.flush()

        for _ in range(1 if nphase >= 6 else 0):
          with contextlib.ExitStack() as st:
            def sb(name, shape, nparts=1, dt=F32):
                return Buf(st.enter_context(nc.sbuf_tensor("sb_" + name, list(shape), dt)), nparts, name)
            TB5 = min(512, TOWN)
            ns5 = TB5 // 128
            WN = 256
            gpost = sb("m_gpost", [128, 2 * D])
            P.dma(gpost[:], gpost_d, writes=[gpost])
            o_t = sb("m_ot", [128, D])
            junk = o_t
            oT = sb("m_oT", [128, 16, TB5], ns5, BF16)
            hmT = oT
            wb = [sb("m_w%d" % i, [128, 16, WN], 1, BF16) for i in range(3)]
            wlist = []
            for _b in range(TOWN // TB5):
                for _n in range(D // WN):
                    wlist.append(("w_out", wbf["w_out"][0][:, _n * WN:(_n + 1) * WN]))
                for _f in range(DFF // WN):
                    wlist.append(("w_up", wbf["w_up"][0][:, _f * WN:(_f + 1) * WN]))
                for _n in range(D // WN):
                    for _f4 in range(4):
                        wlist.append(("w_down", wbf["w_down"][0][_f4 * 2048:(_f4 + 1) * 2048, _n * WN:(_n + 1) * WN]))
            issued = [0]

            def need_w(i):
                while issued[0] <= min(i + 2, len(wlist) - 1):
                    j = issued[0]
                    P.dma(wb[j % 3][:], wlist[j][1].rearrange("(c p) n -> p c n", p=128), reads=[wbf[wlist[j][0]][1]], writes=[wb[j % 3]])
                    issued[0] += 1
                return wb[i % 3]
            mix = [sb("m_mix%d" % i, [128, D]) for i in range(ns5)]
            x1 = [sb("m_x1%d" % i, [128, D]) for i in range(ns5)]
            sss = [sb("m_ss%d" % i, [128, 1]) for i in range(4)]
            rss = [sb("m_rs%d" % i, [128, 1]) for i in range(4)]
            actT = sb("m_actT", [128, 64, TB5], 64, BF16)
            rl = [sb("m_rl%d" % i, [128, TB5]) for i in range(2)]
            wk = 0
            sk = 0
            rk = 0
            for blk in range(TOWN // TB5):
                b0 = blk * TB5
                for s in range(ns5):
                    ch = (b0 + s * 128) // 128
                    P.dma(o_t[:], O[b0 + s * 128:b0 + (s + 1) * 128, :], reads=[(Ob, [2 * ch, 2 * ch + 1])], writes=[o_t])
                    P.dma(x1[s][:], x[b0 + s * 128:b0 + (s + 1) * 128, :], writes=[x1[s]])
                    for c4 in range(4):
                        ps = next_ps()
                        for c in range(4):
                            cc = c4 * 4 + c
                            tr(ps[:, c * 128:(c + 1) * 128], o_t[:, cc * 128:(cc + 1) * 128], [o_t], [ps])
                        evac(oT[:, c4 * 4:(c4 + 1) * 4, s * 128:(s + 1) * 128], ps[:, :].rearrange("p (j t) -> p j t", t=128), [ps], [(oT, s)])
                for n in range(D // WN):
                    wt = need_w(wk)
                    wk += 1
                    for s in range(ns5):
                        ps = next_ps()
                        for c in range(16):
                            mm(ps[:, 0:WN], oT[:, c, s * 128:(s + 1) * 128], wt[:, c, :], c == 0, c == 15, [(oT, s), wt], [ps])
                        evac(mix[s][:, n * WN:(n + 1) * WN], ps[:, 0:WN], [ps], [mix[s]])
                for s in range(ns5):
                    ss, rs = sss[sk % 4], rss[sk % 4]
                    sk += 1
                    act(junk[:], mix[s][:], AF.Square, [mix[s]], [junk, ss], accum=ss[:])
                    rstd_from_ss(ss, rs, D, 1e-6)
                    ts("dve", mix[s][:], mix[s][:], rs[:], None, ALU.mult, None, [mix[s], rs], [mix[s]])
                    tt("pool", mix[s][:], mix[s][:], gpost[:, 0:D], ALU.mult, [mix[s], gpost], [mix[s]])
                    tt("dve", x1[s][:], mix[s][:], x1[s][:], ALU.add, [mix[s], x1[s]], [x1[s]])
                    ss, rs = sss[sk % 4], rss[sk % 4]
                    sk += 1
                    act(junk[:], x1[s][:], AF.Square, [x1[s]], [junk, ss], accum=ss[:])
                    rstd_from_ss(ss, rs, D, 1e-6)
                    ts("dve", mix[s][:], x1[s][:], rs[:], None, ALU.mult, None, [x1[s], rs], [mix[s]])
                    for c4 in range(4):
                        ps = next_ps()
                        for c in range(4):
                            cc = c4 * 4 + c
                            tr(ps[:, c * 128:(c + 1) * 128], mix[s][:, cc * 128:(cc + 1) * 128], [mix[s]], [ps])
                        for c in range(4):
                            cc = c4 * 4 + c
                            ts("dve", hmT[:, cc, s * 128:(s + 1) * 128], ps[:, c * 128:(c + 1) * 128], gcols[:, 16 + cc:17 + cc], None,
                               ALU.mult, None, [ps, gcols], [(hmT, s)])
                for fg in range(DFF // WN):
                    wt = need_w(wk)
                    wk += 1
                    for fc in range(WN // 128):
                        ps = next_ps()
                        for c in range(16):
                            mm(ps[:, 0:TB5], wt[:, c, fc * 128:(fc + 1) * 128], hmT[:, c, :], c == 0, c == 15, [hmT, wt], [ps])
                        r_t = rl[rk % 2]
                        rk += 1
                        act(r_t[:], ps[:, 0:TB5], AF.Relu, [ps], [r_t])
                        f = fg * (WN // 128) + fc
                        tt("pool", actT[:, f, :], r_t[:], r_t[:], ALU.mult, [r_t], [(actT, f)])
                for n in range(D // WN):
                    accs = [next_ps() for _ in range(ns5)]
                    for f4 in range(4):
                        wt = need_w(wk)
                        wk += 1
                        for s in range(ns5):
                            for c in range(16):
                                f = f4 * 16 + c
                                mm(accs[s][:, 0:WN], actT[:, f, s * 128:(s + 1) * 128], wt[:, c, :], f == 0, f == 63, [(actT, f), wt], [accs[s]])
                    for s in range(ns5):
                        evac(mix[s][:, n * WN:(n + 1) * WN], accs[s][:, 0:WN], [accs[s]], [mix[s]])
                for s in range(ns5):
                    ss, rs = sss[sk % 4], rss[sk % 4]
                    sk += 1
                    act(junk[:], mix[s][:], AF.Square, [mix[s]], [junk, ss], accum=ss[:])
                    rstd_from_ss(ss, rs, D, 1e-6)
                    ts("dve", mix[s][:], mix[s][:], rs[:], None, ALU.mult, None, [mix[s], rs], [mix[s]])
                    tt("pool", mix[s][:], mix[s][:], gpost[:, D:2 * D], ALU.mult, [mix[s], gpost], [mix[s]])
                    tt("dve", mix[s][:], mix[s][:], x1[s][:], ALU.add, [mix[s], x1[s]], [mix[s]])
                    P.dma(y_out[b0 + s * 128:b0 + (s + 1) * 128, :], mix[s][:], reads=[mix[s]], eng="pool")
            P.flush()
        P.stack.close()
    return nc, P


def _consts():
    c = np.zeros((128, 1024), np.float32)
    idx = np.arange(128)
    c[:, 0:128] = np.eye(128, dtype=np.float32)
    for di in range(2):
        before = (idx[:, None] < idx[None, :]) if di == 0 else (idx[:, None] > idx[None, :])
        beq = before | np.eye(128, dtype=bool)
        base = 128 + 384 * di
        c[:, base:base + 128] = before
        c[:, base + 128:base + 256] = beq
        c[:, base + 256:base + 384] = before.T
    c[:, 896:1024] = 1.0
    return c


def prep_shared(inp, rev=False):
    f = lambda a: np.ascontiguousarray(np.asarray(a, dtype=np.float32))
    bc = lambda v: np.broadcast_to(np.asarray(v, np.float32).reshape(1, -1), (128, np.asarray(v).size))
    col = lambda v: np.asarray(v, np.float32).reshape(-1, 128).T
    sh = {}
    perm = np.arange(RW)
    if rev:
        perm[3072:3136], perm[3136:3200] = np.arange(3136, 3200), np.arange(3072, 3136)
        perm[3200:3264], perm[3264:3328] = np.arange(3264, 3328), np.arange(3200, 3264)
    w_in = np.asarray(inp["w_in"][0], np.float32)
    if rev:
        w_in = np.concatenate([w_in[:, :RW][:, perm], w_in[:, RW:]], axis=1)
    sh["w_in"] = f(w_in)
    sh["w_out"] = f(inp["w_out"][0])
    sh["w_up"] = f(inp["w_up"][0])
    sh["w_down"] = f(inp["w_down"][0])
    sh["consts"] = _consts()
    sh["gcols"] = f(np.concatenate([col(inp["g_pre_mix"][0]), col(inp["g_pre_mlp"][0])], axis=1))
    mp, mn = np.asarray(inp["mu_prev"][0], np.float32), np.asarray(inp["mu_next"][0], np.float32)
    if rev:
        mp, mn = mn[perm], mp[perm]
    sh["mu"] = f(np.concatenate([bc(mp), bc(mn)], axis=1))
    sh["vecs"] = f(np.concatenate([bc(inp[k][0]) for k in ("k_k", "k_a", "r_k", "gn_w", "gn_b", "cln_w", "cln_b")], axis=1))
    sh["gpost"] = f(np.concatenate([bc(inp["g_post_mix"][0]), bc(inp["g_post_mlp"][0])], axis=1))
    lora = np.zeros((2, 128, DR), np.float32)
    misc = np.zeros((2, 128, DR), np.float32)
    for di, sfx in enumerate(("b", "f") if rev else ("f", "b")):
        lora[di, 0:64] = inp["w2_" + sfx][0]
        lora[di, 64:128] = inp["a2_" + sfx][0]
        misc[di, 0:32] = inp["g2"][0][128:160]
        misc[di, 32] = inp["w0_" + sfx][0]
        misc[di, 64] = inp["a0_" + sfx][0]
    sh["lora"] = lora
    sh["misc"] = misc
    sh["g2a"] = f(inp["g2"][0][0:128])
    dw = np.asarray(inp["dw_w"][0], np.float32)
    if rev:
        dw = dw[::-1]
    sh["dwT"] = f(dw.T.reshape(8, 128, CW).transpose(1, 0, 2).reshape(128, 8 * CW))
    sh["dwb"] = f(col(inp["dw_b"][0]))
    return sh


_CACHE = {}


def kernel(**inputs):
    inp = {k: np.asarray(v) for k, v in inputs.items()}
    T = inp["x_prompt"].shape[1]
    if T not in _CACHE:
        _CACHE[T] = build(T, TOWN=T // 2)[0]
    nc = _CACHE[T]
    shs = [prep_shared(inp, False), prep_shared(inp, True)]
    seqs = [inp["x_prompt"][0], inp["x_prompt"][1], inp["x_sample"][0]]
    n = 8
    in_maps = []
    for c in range(n):
        cc = c if c < 6 else 0
        rev = cc % 2
        m = dict(shs[rev])
        xs = np.asarray(seqs[cc // 2], dtype=np.float32)
        m["x"] = np.ascontiguousarray(xs[::-1] if rev else xs)
        in_maps.append(m)
    res = run_bass_kernel_spmd(nc, in_maps, core_ids=list(range(n)))
    ys = []
    for sq in range(3):
        a = np.asarray(res.results[2 * sq]["y"], dtype=np.float32)
        b = np.asarray(res.results[2 * sq + 1]["y"], dtype=np.float32)
        ys.append(np.concatenate([a, b[::-1]], axis=0))
    return (np.stack([ys[0], ys[1]], axis=0), ys[2][None])
```
